# Optimizing a Trainium2 kernel written in Bass

```python
import math
import jax, jax.numpy as jnp
from jax import lax
import numpy as np

D_MODEL = 1024
BATCH = 2
SEQ = 8192
DEPTH = 4
DEC_BATCH = 128
DEC_SEQ = 8
PAST_LEN = 2048
PAGE_SIZE = 128

N_A_LAYERS = DEPTH // 2
N_B_LAYERS = DEPTH - N_A_LAYERS
SSM_WIDTH = D_MODEL
SSM_GROUP = 16
SSM_GROUPS = SSM_WIDTH // SSM_GROUP
SSM_STATE = 64
SSM_CHUNK = 128
DT_MIN = 1e-3
DT_MAX = 1e-1
WINDOWS = (128, 512, 2048)
DILATIONS = (1, 4, 16)
N_GROUPS_B = 3
HEADS_PER_GROUP = 16
HEAD_DIM = 64
ATTN_WIDTH = HEADS_PER_GROUP * HEAD_DIM
Q_WIDTH = N_GROUPS_B * ATTN_WIDTH
ROT_DIM = HEAD_DIM // 4
ROPE_THETA = 500000.0
Q_BLOCK = 128
PLE_DIM = 256
LN_EPS = 1e-5
DN_ALPHA = (2 * DEPTH) ** 0.25
DN_BETA = (8 * DEPTH) ** -0.25
NEG_INF = -1e30

kernel_name = 'yoco_s5_dilated_swa_decoder_step'


def layer_norm(x, g, b):
    xf = x.astype(jnp.float32)
    mu = jnp.mean(xf, axis=-1, keepdims=True)
    var = jnp.mean(jnp.square(xf - mu), axis=-1, keepdims=True)
    y = (xf - mu) * lax.rsqrt(var + LN_EPS) * g.astype(jnp.float32) + b.astype(jnp.float32)
    return y.astype(x.dtype)


def partial_rotary(x, pos):
    half = ROT_DIM // 2
    inv_freq = ROPE_THETA ** (-jnp.arange(0, ROT_DIM, 2, dtype=jnp.float32) / ROT_DIM)
    ang = pos[:, None] * inv_freq[None, :]
    shape = (1, x.shape[1]) + (1,) * (x.ndim - 3) + (half,)
    cos = jnp.cos(ang).reshape(shape)
    sin = jnp.sin(ang).reshape(shape)
    xf = x.astype(jnp.float32)
    x1 = xf[..., :half]
    x2 = xf[..., half:ROT_DIM]
    out = jnp.concatenate([x1 * cos - x2 * sin, x2 * cos + x1 * sin, xf[..., ROT_DIM:]], axis=-1)
    return out.astype(x.dtype)


def _scan_combine(left, right):
    a1, b1 = left
    a2, b2 = right
    return a1 * a2, a2 * b1 + b2


def s5_ssm(u, h0, a_re, a_im, log_dt, b_re, b_im, c_re, c_im, d_skip):
    n, L, _ = u.shape
    f32 = jnp.float32
    A = lax.complex(a_re.astype(f32), a_im.astype(f32))
    dtA = jnp.exp(log_dt.astype(f32))[:, None] * A
    A_bar = jnp.exp(dtA)
    B_bar = ((A_bar - 1.0) / A)[..., None] * lax.complex(b_re.astype(f32), b_im.astype(f32))
    C = lax.complex(c_re.astype(f32), c_im.astype(f32))
    d = d_skip.astype(f32).reshape(SSM_GROUPS, SSM_GROUP)
    chunk = math.gcd(L, SSM_CHUNK)
    n_chunks = L // chunk
    a_pow = jnp.exp(jnp.arange(1, chunk + 1, dtype=f32)[:, None, None] * dtA[None])
    uc = u.astype(f32).reshape(n, n_chunks, chunk, SSM_GROUPS, SSM_GROUP).transpose(1, 0, 2, 3, 4)

    def step(h, u_blk):
        bu = jnp.einsum('nlgm,gpm->nlgp', u_blk.astype(jnp.complex64), B_bar)
        a = jnp.broadcast_to(A_bar, bu.shape)
        _, hs = lax.associative_scan(_scan_combine, (a, bu), axis=1)
        hs = hs + a_pow[None] * h[:, None]
        y = jnp.einsum('nlgp,gmp->nlgm', hs, C).real + d * u_blk
        return hs[:, -1], y

    h_last, ys = lax.scan(step, h0, uc)
    y = ys.transpose(1, 0, 2, 3, 4).reshape(n, L, SSM_WIDTH)
    return y, h_last


def mixer_a(x, h0, w_in, a_re, a_im, log_dt, b_re, b_im, c_re, c_im, d_skip, w_glu, w_out):
    f32 = jnp.float32
    uz = x @ w_in
    u = uz[..., :SSM_WIDTH]
    z = uz[..., SSM_WIDTH:]
    y, h_last = s5_ssm(u, h0, a_re, a_im, log_dt, b_re, b_im, c_re, c_im, d_skip)
    g = jax.nn.gelu(y)
    glu = g * jax.nn.sigmoid(g @ w_glu.astype(f32))
    out = (glu * jax.nn.silu(z.astype(f32))).astype(x.dtype) @ w_out
    return out, h_last


def group_attend(q, kv, q_idx, k_start, window, dil):
    j = jnp.arange(window // dil + 1)
    idx = q_idx[:, None] - j[None, :] * dil
    valid = (idx >= 0) & (idx + k_start >= 0)
    kvg = jnp.take(kv, jnp.maximum(idx, 0), axis=1)
    kg = kvg[:, :, :, 0].astype(jnp.float32)
    vg = kvg[:, :, :, 1].astype(jnp.float32)
    s = jnp.einsum('nqhd,nqkhd->nhqk', q.astype(jnp.float32), kg) * (HEAD_DIM ** -0.5)
    s = jnp.where(valid[None, None], s, NEG_INF)
    m = jnp.max(s, axis=-1)
    p = jnp.exp(s - m[..., None])
    l = jnp.sum(p, axis=-1)
    o = jnp.einsum('nhqk,nqkhd->nqhd', p, vg) / jnp.transpose(l, (0, 2, 1))[..., None]
    return o, m, l


def combine_groups(parts):
    o = jnp.stack([pt[0] for pt in parts])
    m = jnp.stack([pt[1] for pt in parts])
    l = jnp.stack([pt[2] for pt in parts])
    w = jnp.exp(m - jnp.max(m, axis=0)) * l
    alpha = jnp.transpose(w / jnp.sum(w, axis=0), (0, 1, 3, 2))[..., None]
    return jnp.sum(alpha * o, axis=0)


def dilated_attention_prompt(q, kv_groups):
    n, S = q.shape[:2]
    qs = [q[:, :, g] for g in range(N_GROUPS_B)]
    kv_pad = [jnp.pad(kv, ((0, 0), (W, 0), (0, 0), (0, 0), (0, 0))) for kv, W in zip(kv_groups, WINDOWS)]

    def block(s0):
        parts = []
        for g, (W, dil) in enumerate(zip(WINDOWS, DILATIONS)):
            qb = lax.dynamic_slice_in_dim(qs[g], s0, Q_BLOCK, axis=1)
            kvb = lax.dynamic_slice_in_dim(kv_pad[g], s0, W + Q_BLOCK, axis=1)
            parts.append(group_attend(qb, kvb, W + jnp.arange(Q_BLOCK), s0 - W, W, dil))
        return combine_groups(parts)

    out = lax.map(block, jnp.arange(S // Q_BLOCK, dtype=jnp.int32) * Q_BLOCK)
    return out.transpose(1, 0, 2, 3, 4).reshape(n, S, HEADS_PER_GROUP, HEAD_DIM)


def dilated_attention_sample(q, kv_new, caches):
    ds = q.shape[1]
    parts = []
    for g, (W, dil) in enumerate(zip(WINDOWS, DILATIONS)):
        cache = caches[g]
        lc = cache.shape[1]
        kv_all = jnp.concatenate([cache, kv_new[g]], axis=1)
        parts.append(group_attend(q[:, :, g], kv_all, lc + jnp.arange(ds), PAST_LEN - lc, W, dil))
    return combine_groups(parts)


def shared_kv(x, pos, w_kv):
    n, L, _ = x.shape
    kv = (x @ w_kv).reshape(n, L, 2, N_GROUPS_B, HEADS_PER_GROUP, HEAD_DIM)
    k = partial_rotary(kv[:, :, 0], pos)
    kv = jnp.stack([k, kv[:, :, 1]], axis=2)
    return [kv[:, :, :, g] for g in range(N_GROUPS_B)]


def mixer_b(x, pos, kv_groups, caches, w_in, w_out):
    n, L, _ = x.shape
    qz = x @ w_in
    q = partial_rotary(qz[..., :Q_WIDTH].reshape(n, L, N_GROUPS_B, HEADS_PER_GROUP, HEAD_DIM), pos)
    z = qz[..., Q_WIDTH:]
    if caches is None:
        o = dilated_attention_prompt(q, kv_groups)
    else:
        o = dilated_attention_sample(q, kv_groups, caches)
    gated = o.reshape(n, L, ATTN_WIDTH) * jax.nn.silu(z.astype(jnp.float32))
    return gated.astype(x.dtype) @ w_out


def post_layer(x, sub, p_i, g, b, w_pe_i, w_pg_i):
    h = layer_norm(DN_ALPHA * x + sub, g, b)
    gate = jax.nn.sigmoid((h @ w_pg_i).astype(jnp.float32))
    return (h.astype(jnp.float32) + gate * (p_i @ w_pe_i).astype(jnp.float32)).astype(x.dtype)


def setup_inputs(seed: int = 0) -> dict:
    key = jax.random.key(seed)
    k = jax.random.split(key, 26)
    f32 = jnp.float32

    def nrm(kk, shape, scale=1.0):
        return scale * jax.random.normal(kk, shape, f32)

    G, P, M = SSM_GROUPS, SSM_STATE, SSM_GROUP
    kv_shape = lambda W: (DEC_BATCH, min(W, PAST_LEN), 2, HEADS_PER_GROUP, HEAD_DIM)
    return {
        'x_prompt': nrm(k[0], (BATCH, SEQ, D_MODEL)),
        'x_sample': nrm(k[1], (DEC_BATCH, DEC_SEQ, D_MODEL)),
        'state_ssm_re': nrm(k[2], (N_A_LAYERS, DEC_BATCH, G, P), 0.5),
        'state_ssm_im': nrm(k[3], (N_A_LAYERS, DEC_BATCH, G, P), 0.5),
        'cache_kv_w128': nrm(k[4], kv_shape(WINDOWS[0])),
        'cache_kv_w512': nrm(k[5], kv_shape(WINDOWS[1])),
        'cache_kv_w2048': nrm(k[6], kv_shape(WINDOWS[2])),
        'p_prompt': nrm(k[7], (DEPTH, BATCH, SEQ, PLE_DIM)),
        'p_sample': nrm(k[8], (DEPTH, DEC_BATCH, DEC_SEQ, PLE_DIM)),
        'ln_g': 1.0 + nrm(k[9], (DEPTH, D_MODEL), 0.02),
        'ln_b': nrm(k[10], (DEPTH, D_MODEL), 0.02),
        'w_pe': nrm(k[11], (DEPTH, PLE_DIM, D_MODEL), PLE_DIM ** -0.5),
        'w_pg': nrm(k[12], (DEPTH, D_MODEL, D_MODEL), D_MODEL ** -0.5),
        'w_in_a': nrm(k[13], (N_A_LAYERS, D_MODEL, 2 * SSM_WIDTH), D_MODEL ** -0.5),
        'a_re': -0.5 + nrm(k[14], (N_A_LAYERS, G, P), 0.01),
        'a_im': jnp.broadcast_to(jnp.pi * jnp.arange(P, dtype=f32), (N_A_LAYERS, G, P)),
        'log_dt': jax.random.uniform(k[15], (N_A_LAYERS, G), f32, minval=math.log(DT_MIN), maxval=math.log(DT_MAX)),
        'b_re': nrm(k[16], (N_A_LAYERS, G, P, M), (2 * M) ** -0.5),
        'b_im': nrm(k[17], (N_A_LAYERS, G, P, M), (2 * M) ** -0.5),
        'c_re': nrm(k[18], (N_A_LAYERS, G, M, P), P ** -0.5),
        'c_im': nrm(k[19], (N_A_LAYERS, G, M, P), P ** -0.5),
        'd_skip': nrm(k[20], (N_A_LAYERS, SSM_WIDTH)),
        'w_glu': nrm(k[21], (N_A_LAYERS, SSM_WIDTH, SSM_WIDTH), SSM_WIDTH ** -0.5),
        'w_out_a': nrm(k[22], (N_A_LAYERS, SSM_WIDTH, D_MODEL), DN_BETA * SSM_WIDTH ** -0.5),
        'w_kv': nrm(k[23], (D_MODEL, 2 * Q_WIDTH), D_MODEL ** -0.5),
        'w_in_b': nrm(k[24], (N_B_LAYERS, D_MODEL, Q_WIDTH + ATTN_WIDTH), D_MODEL ** -0.5),
        'w_out_b': nrm(k[25], (N_B_LAYERS, ATTN_WIDTH, D_MODEL), DN_BETA * ATTN_WIDTH ** -0.5),
    }


def reference(x_prompt, x_sample, state_ssm_re, state_ssm_im, cache_kv_w128, cache_kv_w512, cache_kv_w2048,
              p_prompt, p_sample, ln_g, ln_b, w_pe, w_pg, w_in_a, a_re, a_im, log_dt, b_re, b_im, c_re, c_im,
              d_skip, w_glu, w_out_a, w_kv, w_in_b, w_out_b):
    f32 = jnp.float32

    def run(x, p, h0, caches, pos):
        h_last = []
        kv_groups = None
        for i in range(DEPTH):
            if i < N_A_LAYERS:
                sub, hl = mixer_a(x, h0[i], w_in_a[i], a_re[i], a_im[i], log_dt[i], b_re[i], b_im[i],
                                  c_re[i], c_im[i], d_skip[i], w_glu[i], w_out_a[i])
                h_last.append(hl)
            else:
                if kv_groups is None:
                    kv_groups = shared_kv(x, pos, w_kv)
                j = i - N_A_LAYERS
                sub = mixer_b(x, pos, kv_groups, caches, w_in_b[j], w_out_b[j])
            x = post_layer(x, sub, p[i], ln_g[i], ln_b[i], w_pe[i], w_pg[i])
        return x, jnp.stack(h_last), kv_groups

    seq = x_prompt.shape[1]
    pos_p = jnp.arange(seq, dtype=f32)
    pos_s = PAST_LEN + jnp.arange(x_sample.shape[1], dtype=f32)
    h0_p = jnp.zeros((N_A_LAYERS, x_prompt.shape[0], SSM_GROUPS, SSM_STATE), jnp.complex64)
    h0_s = lax.complex(state_ssm_re.astype(f32), state_ssm_im.astype(f32))

    y_prompt, h_p, kv_p = run(x_prompt, p_prompt, h0_p, None, pos_p)
    y_sample, h_s, kv_s = run(x_sample, p_sample, h0_s, (cache_kv_w128, cache_kv_w512, cache_kv_w2048), pos_s)

    kv_w128_prompt = kv_p[0][:, seq - min(WINDOWS[0], seq):]
    kv_w512_prompt = kv_p[1][:, seq - min(WINDOWS[1], seq):]
    kv_w2048_prompt = kv_p[2][:, seq - min(WINDOWS[2], seq):]
    kv_w128_sample = kv_s[0]
    kv_w512_sample = kv_s[1]
    kv_w2048_sample = kv_s[2]
    return (y_prompt, y_sample, h_p.real, h_p.imag, h_s.real, h_s.imag,
            kv_w128_prompt, kv_w512_prompt, kv_w2048_prompt,
            kv_w128_sample, kv_w512_sample, kv_w2048_sample)
```

```python
import math
import numpy as np
import ml_dtypes
import concourse.bass as bass
import concourse.mybir as mybir
from concourse.bass_utils import run_bass_kernel_spmd

F32 = mybir.dt.float32
BF16 = mybir.dt.bfloat16
AF = mybir.ActivationFunctionType
ALU = mybir.AluOpType
AX = mybir.AxisListType

D = 1024
DEPTH = 4
NA = 2
G = 64
P = 64
M = 16
PLE = 256
LN_EPS = 1e-5
DN_ALPHA = (2 * DEPTH) ** 0.25
PI = math.pi
GSC = 16384.0


class Res:
    __slots__ = ("name", "w", "r")

    def __init__(self, name=""):
        self.name = name
        self.w = None
        self.r = []


class KB:
    ENG = ("pe", "act", "dve", "pool", "sp")

    def __init__(self, nc, n_dma_sems=16):
        self.nc = nc
        self.ops = {e: [] for e in self.ENG}
        self.cnt = {e: 0 for e in self.ENG}
        self.sems = {}
        for e in self.ENG:
            self.sems[e] = nc.semaphore("s_" + e).__enter__()
        self.n_dma = n_dma_sems
        for i in range(n_dma_sems):
            self.sems["d%d" % i] = nc.semaphore("s_d%d" % i).__enter__()
        self.dma_val = [0] * n_dma_sems
        self.dma_rr = 0
        self.waited = {e: {} for e in self.ENG}
        self.epoch = 0
        self.ack = [nc.semaphore("s_ack%d" % i).__enter__() for i in range(2)]
        self.n_reset = 0
        self.RESET_AT = 24000

    def maybe_reset(self):
        if max(self.cnt.values()) >= self.RESET_AT or max(self.dma_val) >= self.RESET_AT:
            self.reset()

    def reset(self):
        self.barrier()
        self.n_reset += 1
        k = self.n_reset
        sems, ack = self.sems, self.ack
        ne = len(self.ENG)
        for eng in self.ENG:
            def run(h, eng=eng, k=k):
                h.sem_inc(ack[0], 1)
                h.wait_ge(ack[0], ne * k)
                h.sem_clear(sems[eng])
                if eng == "sp":
                    for i in range(self.n_dma):
                        h.sem_clear(sems["d%d" % i])
                h.sem_inc(ack[1], 1)
                h.wait_ge(ack[1], ne * k)
            self.ops[eng].append(run)
        self.epoch += 1
        self.cnt = {e: 0 for e in self.ENG}
        self.dma_val = [0] * self.n_dma
        self.waited = {e: {} for e in self.ENG}

    def _deps(self, eng, reads, writes):
        deps = {}

        def add(d):
            if d is None:
                return
            k, v = d
            if deps.get(k, 0) < v:
                deps[k] = v
        ep = self.epoch
        for t in reads:
            if t.w is not None and t.w[2] == ep:
                add(t.w[:2])
        for t in writes:
            if t.w is not None and t.w[2] == ep:
                add(t.w[:2])
            for d in t.r:
                if d[2] == ep:
                    add(d[:2])
        out = []
        wd = self.waited[eng]
        if eng == "pe":
            deps.pop("pe", None)
        for k, v in deps.items():
            if wd.get(k, 0) < v:
                wd[k] = v
                out.append((k, v))
        return out

    def _mark(self, me, reads, writes):
        me = (me[0], me[1], self.epoch)
        for t in reads:
            t.r.append(me)
            if len(t.r) > 64:
                best = {}
                for k, v, e in t.r:
                    if e == self.epoch and best.get(k, 0) < v:
                        best[k] = v
                t.r = [(k, v, self.epoch) for k, v in best.items()]
        for t in writes:
            t.w = me
            t.r = []

    def op(self, eng, fn, reads=(), writes=()):
        self.maybe_reset()
        waits = self._deps(eng, reads, writes)
        self.cnt[eng] += 1
        me = (eng, self.cnt[eng])
        sems = self.sems

        def run(h, waits=waits, fn=fn, eng=eng):
            for k, v in waits:
                h.wait_ge(sems[k], v)
            fn(h).then_inc(sems[eng], 1)
        self.ops[eng].append(run)
        self._mark(me, reads, writes)

    def dma(self, eng, out, in_, reads=(), writes=(), **kw):
        self.maybe_reset()
        i = self.dma_rr
        self.dma_rr = (self.dma_rr + 1) % self.n_dma
        key = "d%d" % i
        waits = self._deps(eng, reads, writes)
        prev = self.dma_val[i]
        wd = self.waited[eng]
        if prev > 0 and wd.get(key, 0) < prev:
            wd[key] = prev
            waits.append((key, prev))
        self.dma_val[i] = prev + 16
        me = (key, prev + 16)
        sems = self.sems

        def run(h, waits=waits, out=out, in_=in_, key=key, kw=kw):
            for k, v in waits:
                h.wait_ge(sems[k], v)
            h.dma_start(out=out, in_=in_, **kw).then_inc(sems[key], 16)
        self.ops[eng].append(run)
        self._mark(me, reads, writes)

    def barrier(self):
        sems = self.sems
        snap = [(e, self.cnt[e]) for e in self.ENG if self.cnt[e] > 0]
        snap += [("d%d" % i, v) for i, v in enumerate(self.dma_val) if v > 0]
        for eng in self.ENG:
            wd = self.waited[eng]
            waits = []
            for k, v in snap:
                if k != eng and wd.get(k, 0) < v:
                    wd[k] = v
                    waits.append((k, v))

            def run(h, waits=waits):
                for k, v in waits:
                    h.wait_ge(sems[k], v)
            self.ops[eng].append(run)

    def finish(self, out_res):
        waits = self._deps("sp", out_res, ())
        sems = self.sems

        def run(h, waits=waits):
            for k, v in waits:
                h.wait_ge(sems[k], v)
        self.ops["sp"].append(run)

    def _clear_all(self):
        nc = self.nc
        allsems = list(self.sems.values()) + list(getattr(self, "ack", []))
        nc.all_engine_barrier()
        with nc.Block() as blk:
            @blk.gpsimd
            def _(h):
                for sm in allsems:
                    h.sem_clear(sm)
        nc.all_engine_barrier()

    def emit(self):
        nc = self.nc
        ops = self.ops
        self._clear_all()
        self._emit_main()
        self._clear_all()

    def _emit_main(self):
        nc = self.nc
        ops = self.ops
        with nc.Block() as block:
            @block.tensor
            def _(h):
                for f in ops["pe"]:
                    f(h)

            @block.scalar
            def _(h):
                for f in ops["act"]:
                    f(h)

            @block.vector
            def _(h):
                for f in ops["dve"]:
                    f(h)

            @block.gpsimd
            def _(h):
                for f in ops["pool"]:
                    f(h)

            @block.sync
            def _(h):
                for f in ops["sp"]:
                    f(h)


class Buf:
    def __init__(self, t, name):
        self.t = t
        self.res = Res(name)

    def __getitem__(self, k):
        return self.t[k]


class Builder:
    def __init__(self, L, NS, dbg=False):
        self.L = L
        self.NS = NS
        self.dbg = dbg
        self.nc = nc = bass.Bass("TRN2", target_bir_lowering=False)
        self.kb = KB(nc)
        self.dram = {}
        self.dres = {}
        self.rr = 0
        self.es = None
        self.uid = 0

    def din(self, name, shape, dt=F32):
        self.dram[name] = self.nc.dram_tensor(name, list(shape), dt, kind="ExternalInput").ap()
        self.dres[name] = Res(name)
        return self.dram[name]

    def dout(self, name, shape, dt=F32):
        self.dram[name] = self.nc.dram_tensor(name, list(shape), dt, kind="ExternalOutput").ap()
        self.dres[name] = Res(name)
        return self.dram[name]

    def dscr(self, name, shape, dt):
        self.dram[name] = self.nc.dram_tensor(name, list(shape), dt).ap()
        self.dres[name] = Res(name)
        return self.dram[name]

    def sb(self, name, shape, dt):
        if self.es is not None:
            self.uid += 1
            t = self.es.enter_context(self.nc.sbuf_tensor("%s_%d" % (name, self.uid), list(shape), dt))
            return Buf(t, name)
        return Buf(self.nc.alloc_sbuf_tensor(name, list(shape), dt), name)

    def any_eng(self):
        self.rr += 1
        return ("dve", "act")[self.rr % 2]

    def copy(self, eng, out, in_, reads, writes):
        if eng == "act":
            self.kb.op("act", lambda h: h.copy(out=out, in_=in_), reads, writes)
        else:
            self.kb.op(eng, lambda h: h.tensor_copy(out=out, in_=in_), reads, writes)

    def act(self, out, in_, func, reads, writes, scale=1.0, **kw):
        self.kb.op("act", lambda h: h.activation(out=out, in_=in_, func=func, scale=scale, **kw), reads, writes)

    def tt(self, eng, out, in0, in1, op, reads, writes):
        self.kb.op(eng, lambda h: h.tensor_tensor(out=out, in0=in0, in1=in1, op=op), reads, writes)

    def ts(self, eng, out, in0, s1, s2, op0, op1, reads, writes):
        if s2 is None:
            self.kb.op(eng, lambda h: h.tensor_scalar(out=out, in0=in0, scalar1=s1, scalar2=None, op0=op0), reads, writes)
        else:
            self.kb.op(eng, lambda h: h.tensor_scalar(out=out, in0=in0, scalar1=s1, scalar2=s2, op0=op0, op1=op1), reads, writes)

    def stt(self, eng, out, in0, scalar, in1, op0, op1, reads, writes):
        self.kb.op(eng, lambda h: h.scalar_tensor_tensor(out=out, in0=in0, scalar=scalar, in1=in1, op0=op0, op1=op1), reads, writes)

    def mm(self, out, lhsT, rhs, start, stop, reads, writes):
        self.kb.op("pe", lambda h: h.matmul(out, lhsT=lhsT, rhs=rhs, start=start, stop=stop), reads, writes)

    def tr(self, out, in_, ident, reads, writes):
        self.kb.op("pe", lambda h: h.transpose(out=out, in_=in_, identity=ident), reads, writes)

    def ld(self, out, in_, reads, writes, eng="sp", **kw):
        self.kb.dma(eng, out, in_, reads, writes, **kw)


def build(L, NS, dbg=False):
    import os as _os
    _astop = int(_os.environ.get("K_ASTOP", "99"))
    _pstop = int(_os.environ.get("K_PSTOP", "99"))
    from contextlib import ExitStack
    B = Builder(L, NS, dbg)
    nc, kb = B.nc, B.kb
    NT = L // 1024
    NSTOK = NS * 8
    xp = B.din("xp", [L, D])
    xs = B.din("xs", [NSTOK, D])
    pp = B.din("pp", [DEPTH, L, PLE])
    ps_ = B.din("ps", [DEPTH, NSTOK, PLE])
    st_re = B.din("st_re", [NA, NS, G, P])
    st_im = B.din("st_im", [NA, NS, G, P])
    ln_g = B.din("ln_g", [DEPTH, D])
    ln_b = B.din("ln_b", [DEPTH, D])
    w_pe = B.din("w_pe", [DEPTH, PLE, D])
    w_pg = B.din("w_pg", [DEPTH, D, D])
    w_in_a = B.din("w_in_a", [NA, D, 2 * D])
    a_re = B.din("a_re", [NA, G, P])
    a_im = B.din("a_im", [NA, G, P])
    log_dt = B.din("log_dt", [NA, G])
    b_re = B.din("b_re", [NA, G, P, M])
    b_im = B.din("b_im", [NA, G, P, M])
    c_re = B.din("c_re", [NA, G, M, P])
    c_im = B.din("c_im", [NA, G, M, P])
    d_skip = B.din("d_skip", [NA, D])
    w_glu = B.din("w_glu", [NA, D, D])
    w_out_a = B.din("w_out_a", [NA, D, D])
    w_kv = B.din("w_kv", [D, 6144])
    w_in_b = B.din("w_in_b", [2, D, 4096])
    w_qsw = B.din("w_qsw", [2, D, 3072])
    w_out_b = B.din("w_out_b", [2, D, D])
    rcp = B.din("rcp", [128, NT * 64])
    rsp = B.din("rsp", [128, NT * 64])
    rcs = B.din("rcs", [128, 64])
    rss = B.din("rss", [128, 64])
    rqc = B.din("rqc", [128, L])
    rqs = B.din("rqs", [128, L])
    rqcs = B.din("rqcs", [128, 8])
    rqss = B.din("rqss", [128, 8])
    c_mask = B.din("c_mask", [128, 256])
    c_smask = B.din("c_smask", [16, 1664 + 24])
    cch = [B.din("c128", [NS, 128, 2, 16, 64]), B.din("c512", [NS, 512, 2, 16, 64]), B.din("c2048", [NS, 2048, 2, 16, 64])]
    KTs = B.dscr("KTs", [NS, 8, 128, 1664], BF16)
    Vs = B.dscr("Vs", [NS, 13, 128, 1024], BF16)
    KTnd = B.dscr("KTnd", [3, 8, 128, 128], BF16)
    Vnd = B.dscr("Vnd", [NSTOK, 3072], BF16)
    c_ident = B.din("c_ident", [128, 128])
    c_m1 = B.din("c_m1", [128, 128])
    c_sgn = B.din("c_sgn", [128, 1])
    c_iota = B.din("c_iota", [128, 65])
    yp = B.dout("yp", [L, D])
    ys = B.dout("ys", [NSTOK, D])
    hp_o = B.dout("hp", [NA, G, 128])
    hs_o = B.dout("hs", [NA, NS, G, 128])
    WS = (128, 512, 2048)
    kvp_o = [B.dout("kvp%d" % g, [min(WS[g], L), 2, 16, 64]) for g in range(3)]
    kvs_o = [B.dout("kvs%d" % g, [NSTOK, 2, 16, 64]) for g in range(3)]
    KTd = B.dscr("KTd", [3, 8, 128, L], BF16)
    Vd = B.dscr("Vd", [L, 3072], BF16)
    Xd = B.dscr("Xd", [L + NSTOK, D], F32)
    wb = {}
    for nm, shp in (("w_in_a", [NA, D, 2 * D]), ("w_glu", [NA, D, D]), ("w_out_a", [NA, D, D]),
                    ("w_pg", [DEPTH, D, D]), ("w_pe", [DEPTH, PLE, D]), ("w_kv", [1, D, 6144]),
                    ("w_in_b", [2, D, 4096]), ("w_qsw", [2, D, 3072]), ("w_out_b", [2, D, D])):
        wb[nm] = B.dscr(nm + "_b", shp, BF16)
    W1d = B.dscr("W1d", [128, G, 128], BF16)
    W2d = B.dscr("W2d", [128, G, 128], BF16)
    W4d = B.dscr("W4d", [128, G, 128], BF16)
    TCd = B.dscr("TCd", [128, G, 65], F32)
    TSd = B.dscr("TSd", [128, G, 65], F32)
    RHOd = B.dscr("RHOd", [128, G, 64], F32)
    dr = B.dres

    ident_f = B.sb("ident_f", [128, 128], F32)
    ident_b = B.sb("ident_b", [128, 128], BF16)
    m1 = B.sb("m1", [128, 128], F32)
    sgn = B.sb("sgn", [128, 1], F32)
    iota = B.sb("iota", [128, 65], F32)
    A8R = B.sb("A8R", [128, G], F32)
    A8I = B.sb("A8I", [128, G], F32)
    lng = B.sb("lng", [128, D], F32)
    lnb = B.sb("lnb", [128, D], F32)
    B.ld(ident_f[:, :], c_ident[:, :], [], [ident_f.res])
    B.ld(m1[:, :], c_m1[:, :], [], [m1.res])
    B.ld(sgn[:, :], c_sgn[:, :], [], [sgn.res])
    B.ld(iota[:, :], c_iota[:, :], [], [iota.res])
    B.copy("dve", ident_b[:, :], ident_f[:, :], [ident_f.res], [ident_b.res])

    psb = [Buf(nc.psum_tensor("psb%d" % i, [128, 1024], BF16).__enter__(), "psb%d" % i) for i in range(2)]
    psf = [Buf(nc.psum_tensor("psf%d" % i, [128, 512], F32).__enter__(), "psf%d" % i) for i in range(6)]
    cnt = {"psb": 0, "psf": 0}

    def next_psb():
        cnt["psb"] += 1
        return psb[cnt["psb"] % 2]

    def next_psf(lo=0, hi=4):
        cnt["psf"] += 1
        return psf[lo + cnt["psf"] % (hi - lo)]

    with ExitStack() as es:
        B.es = es
        cvt = [B.sb("cvt", [128, 2048], F32) for i in range(2)]
        cvb = [B.sb("cvb", [128, 2048], BF16) for i in range(2)]
        ci = 0
        for nm, src in (("w_in_a", w_in_a), ("w_glu", w_glu), ("w_out_a", w_out_a), ("w_pg", w_pg), ("w_pe", w_pe),
                        ("w_kv", w_kv), ("w_in_b", w_in_b), ("w_qsw", w_qsw), ("w_out_b", w_out_b)):
            s2 = src.rearrange("l k n -> (l k) n") if len(src.shape) == 3 else src
            d2 = wb[nm].rearrange("l k n -> (l k) n")
            rows, cols = s2.shape
            for r0 in range(0, rows, 128):
                for c0 in range(0, cols, 2048):
                    cw = min(2048, cols - c0)
                    f, b = cvt[ci % 2], cvb[ci % 2]
                    B.ld(f[:, 0:cw], s2[r0:r0 + 128, c0:c0 + cw], [], [f.res], eng=("sp", "act")[ci % 2])
                    B.copy(("dve", "pool")[ci % 2], b[:, 0:cw], f[:, 0:cw], [f.res], [b.res])
                    B.ld(d2[r0:r0 + 128, c0:c0 + cw], b[:, 0:cw], [b.res], [dr[nm + "_b"]], eng=("sp", "act")[ci % 2])
                    ci += 1
        kb.barrier()
    B.es = None

    def gen_tables(li):
        T = lambda name, shape, dt=F32: B.sb(name, shape, dt)
        Bst = T("Bst", [128, G, M]); Bsw = T("Bsw", [128, G, M])
        Cst = T("Cst", [128, G, M]); Csw = T("Csw", [128, G, M])
        bre = b_re[li].rearrange("g p m -> p g m"); bim = b_im[li].rearrange("g p m -> p g m")
        B.ld(Bst[0:64, :, :], bre, [], [Bst.res]); B.ld(Bst[64:128, :, :], bim, [], [Bst.res])
        B.ld(Bsw[0:64, :, :], bim, [], [Bsw.res]); B.ld(Bsw[64:128, :, :], bre, [], [Bsw.res])
        cin = T("cin", [128, 8, 256])
        cre = c_re[li].rearrange("(gt g8) m p -> (g8 m) gt p", g8=8)
        cim = c_im[li].rearrange("(gt g8) m p -> (g8 m) gt p", g8=8)
        B.ld(cin[:, :, 0:64], cre, [], [cin.res]); B.ld(cin[:, :, 64:128], cim, [], [cin.res])
        B.ld(cin[:, :, 128:192], cim, [], [cin.res]); B.ld(cin[:, :, 192:256], cre, [], [cin.res])
        for gt in range(8):
            for j, dst in enumerate((Cst, Csw)):
                pz = next_psf()
                B.tr(pz[:, 0:128], cin[:, gt, j * 128:(j + 1) * 128], ident_f[:, :], [cin.res, ident_f.res], [pz.res])
                B.copy(B.any_eng(), dst[:, gt * 8:(gt + 1) * 8, :], pz[:, 0:128].rearrange("p (g m) -> p g m", m=M), [pz.res], [dst.res])
        ain = T("ain", [64, 256])
        B.ld(ain[:, 0:64], a_re[li], [], [ain.res]); B.ld(ain[:, 64:128], a_re[li], [], [ain.res])
        B.ld(ain[:, 128:192], a_im[li], [], [ain.res]); B.ld(ain[:, 192:256], a_im[li], [], [ain.res])
        ARE = T("ARE", [128, G]); AIM = T("AIM", [128, G])
        for j, dst in enumerate((ARE, AIM)):
            pz = next_psf()
            B.tr(pz[:, 0:64], ain[:, j * 128:(j + 1) * 128], ident_f[0:64, 0:64], [ain.res, ident_f.res], [pz.res])
            B.copy("dve", dst[:, :], pz[:, 0:64], [pz.res], [dst.res])
        dt_ = T("dt", [128, G])
        B.ld(dt_[:, :], log_dt[li:li + 1, :].partition_broadcast(128).rearrange("p o g -> p (o g)"), [], [dt_.res])
        B.act(dt_[:, :], dt_[:, :], AF.Exp, [dt_.res], [dt_.res])
        lre = T("lre", [128, G]); lim = T("lim", [128, G])
        B.tt("dve", lre[:, :], ARE[:, :], dt_[:, :], ALU.mult, [ARE.res, dt_.res], [lre.res])
        B.tt("dve", lim[:, :], AIM[:, :], dt_[:, :], ALU.mult, [AIM.res, dt_.res], [lim.res])
        PE_ = T("PE", [128, 9, G]); PC = T("PC", [128, 9, G]); PS = T("PS", [128, 9, G]); PEi = T("PEi", [128, 9, G])
        rr_i = T("rr_i", [128, 32 * 65], mybir.dt.int32)
        rr_f = T("rr_f", [128, 32 * 65], F32)
        rr_m = T("rr_m", [128, 32 * 65], F32)

        def rr(buf, ap2d, n):
            ti, tf, tm = rr_i[:, 0:n], rr_f[:, 0:n], rr_m[:, 0:n]
            B.ts("dve", tf, ap2d, 1.0 / (2 * PI), None, ALU.mult, None, [buf.res], [rr_f.res])
            B.copy("dve", ti, tf, [rr_f.res], [rr_i.res])
            B.copy("dve", tf, ti, [rr_i.res], [rr_f.res])
            B.stt("dve", ap2d, tf, -2 * PI, ap2d, ALU.mult, ALU.add, [rr_f.res, buf.res], [buf.res])
            B.ts("dve", ap2d, ap2d, -PI, None, ALU.add, None, [buf.res], [buf.res])
            B.ts("dve", tm, ap2d, -PI, None, ALU.is_lt, None, [buf.res], [rr_m.res])
            B.stt("dve", ap2d, tm, 2 * PI, ap2d, ALU.mult, ALU.add, [rr_m.res, buf.res], [buf.res])
            B.ts("dve", tm, ap2d, PI, None, ALU.is_gt, None, [buf.res], [rr_m.res])
            B.stt("dve", ap2d, tm, -2 * PI, ap2d, ALU.mult, ALU.add, [rr_m.res, buf.res], [buf.res])
            B.ts("dve", ap2d, ap2d, -PI, PI, ALU.max, ALU.min, [buf.res], [buf.res])
        ang = T("ang", [128, 9, G])
        for n in range(9):
            B.act(PE_[:, n, :], lre[:, :], AF.Exp, [lre.res], [PE_.res], scale=float(n))
            B.act(PEi[:, n, :], lre[:, :], AF.Exp, [lre.res], [PEi.res], scale=-float(n))
            B.ts("dve", ang[:, n, :], lim[:, :], float(n), PI, ALU.mult, ALU.add, [lim.res], [ang.res])
        rr(ang, ang[:, :, :].rearrange("p n g -> p (n g)"), 9 * G)
        B.act(PS[:, :, :], ang[:, :, :], AF.Sin, [ang.res], [PS.res])
        for n in range(9):
            B.ts("dve", ang[:, n, :], lim[:, :], float(n), 1.5 * PI, ALU.mult, ALU.add, [lim.res, PS.res], [ang.res])
        rr(ang, ang[:, :, :].rearrange("p n g -> p (n g)"), 9 * G)
        B.act(PC[:, :, :], ang[:, :, :], AF.Sin, [ang.res], [PC.res])
        nr = T("nr", [128, G]); ni = T("ni", [128, G]); t0 = T("t0", [128, G]); t1 = T("t1", [128, G])
        qre = T("qre", [128, G]); qim = T("qim", [128, G]); den = T("den", [128, G])
        B.tt("dve", nr[:, :], PE_[:, 1, :], PC[:, 1, :], ALU.mult, [PE_.res, PC.res], [nr.res])
        B.ts("dve", nr[:, :], nr[:, :], -1.0, None, ALU.add, None, [nr.res], [nr.res])
        B.tt("dve", ni[:, :], PE_[:, 1, :], PS[:, 1, :], ALU.mult, [PE_.res, PS.res], [ni.res])
        B.tt("dve", t0[:, :], ARE[:, :], ARE[:, :], ALU.mult, [ARE.res], [t0.res])
        B.tt("dve", t1[:, :], AIM[:, :], AIM[:, :], ALU.mult, [AIM.res], [t1.res])
        B.tt("dve", den[:, :], t0[:, :], t1[:, :], ALU.add, [t0.res, t1.res], [den.res])
        B.kb.op("dve", lambda h: h.reciprocal(out=den[:, :], in_=den[:, :]), [den.res], [den.res])
        B.tt("dve", t0[:, :], nr[:, :], ARE[:, :], ALU.mult, [nr.res, ARE.res, den.res], [t0.res])
        B.tt("dve", t1[:, :], ni[:, :], AIM[:, :], ALU.mult, [ni.res, AIM.res], [t1.res])
        B.tt("dve", qre[:, :], t0[:, :], t1[:, :], ALU.add, [t0.res, t1.res], [qre.res])
        B.tt("dve", qre[:, :], qre[:, :], den[:, :], ALU.mult, [qre.res, den.res], [qre.res])
        B.tt("dve", t0[:, :], ni[:, :], ARE[:, :], ALU.mult, [ni.res, ARE.res, qre.res], [t0.res])
        B.tt("dve", t1[:, :], nr[:, :], AIM[:, :], ALU.mult, [nr.res, AIM.res], [t1.res])
        B.tt("dve", qim[:, :], t0[:, :], t1[:, :], ALU.subtract, [t0.res, t1.res], [qim.res])
        B.tt("dve", qim[:, :], qim[:, :], den[:, :], ALU.mult, [qim.res, den.res], [qim.res])
        B.tt("dve", A8R[:, :], PE_[:, 8, :], PC[:, 8, :], ALU.mult, [PE_.res, PC.res], [A8R.res])
        B.tt("dve", A8I[:, :], PE_[:, 8, :], PS[:, 8, :], ALU.mult, [PE_.res, PS.res], [A8I.res])
        B.ts("dve", A8I[:, :], A8I[:, :], sgn[:, 0:1], None, ALU.mult, None, [A8I.res, sgn.res], [A8I.res])
        outer_es = B.es
        with ExitStack() as es_r:
            B.es = es_r
            RHO = T("RHO", [128, G, 64])
            B.copy("dve", RHO[:, :, :], PE_[:, 8, :].unsqueeze(2).to_broadcast([128, G, 64]), [PE_.res], [RHO.res])
            B.ld(RHOd[:, :, :], RHO[:, :, :], [RHO.res], [dr["RHOd"]])
            kb.barrier()
        B.es = outer_es
        TC = T("TC", [128, G, 65]); TS = T("TS", [128, G, 65])
        B.tt("dve", TC[:, :, :], lim[:, :].unsqueeze(2).to_broadcast([128, G, 65]),
             iota[:, :].unsqueeze(1).to_broadcast([128, G, 65]), ALU.mult, [lim.res, iota.res], [TC.res])
        B.ts("dve", TS[:, :, :], TC[:, :, :], 8.0, PI, ALU.mult, ALU.add, [TC.res], [TS.res])
        for hh_ in range(2):
            rr(TS, TS[:, hh_ * 32:(hh_ + 1) * 32, :].rearrange("p g c -> p (g c)"), 32 * 65)
        B.act(TS[:, :, :], TS[:, :, :], AF.Sin, [TS.res], [TS.res])
        B.ts("dve", TS[:, :, :], TS[:, :, :], sgn[:, 0:1], -1.0, ALU.mult, ALU.mult, [TS.res, sgn.res], [TS.res])
        B.ts("dve", TC[:, :, :], TC[:, :, :], 8.0, 1.5 * PI, ALU.mult, ALU.add, [TC.res, TS.res], [TC.res])
        for hh_ in range(2):
            rr(TC, TC[:, hh_ * 32:(hh_ + 1) * 32, :].rearrange("p g c -> p (g c)"), 32 * 65)
        B.act(TC[:, :, :], TC[:, :, :], AF.Sin, [TC.res], [TC.res])
        B.ld(TCd[:, :, :], TC[:, :, :], [TC.res], [dr["TCd"]])
        B.ld(TSd[:, :, :], TS[:, :, :], [TS.res], [dr["TSd"]])
        dcol = T("dcol", [128, G])
        dsk = d_skip[li].rearrange("(g m) -> m g", m=M)
        for s in range(8):
            B.ld(dcol[s * 16:(s + 1) * 16, :], dsk, [], [dcol.res], allow_slow_non_contiguous=True)
        kre = T("kre", [128, G]); kim = T("kim", [128, G]); tb0 = T("tb0", [128, 32, M])
        tmpw = T("tmpw", [128, 128])
        GH = 32
        W2T = T("W2T", [128, GH, 8, M]); W2Ts = T("W2Ts", [128, GH, 8, M]); W4 = T("W4", [128, GH, 8, M])
        Wo = [T("Wo%d" % i, [128, GH, 128], BF16) for i in range(3)]

        def cmul_coef(n_idx, neg):
            Cn = PC[:, n_idx, :]; Sn = PS[:, n_idx, :]; En = PE_[:, n_idx, :]
            rd = [PC.res, PS.res, PE_.res, qre.res, qim.res]
            B.tt("dve", t0[:, :], Cn, qre[:, :], ALU.mult, rd, [t0.res])
            B.tt("dve", t1[:, :], Sn, qim[:, :], ALU.mult, rd, [t1.res])
            B.tt("dve", kre[:, :], t0[:, :], t1[:, :], ALU.add if neg else ALU.subtract, [t0.res, t1.res], [kre.res])
            B.tt("dve", t0[:, :], Sn, qre[:, :], ALU.mult, rd + [kre.res], [t0.res])
            B.tt("dve", t1[:, :], Cn, qim[:, :], ALU.mult, rd, [t1.res])
            if neg:
                B.tt("dve", kim[:, :], t1[:, :], t0[:, :], ALU.subtract, [t0.res, t1.res], [kim.res])
                B.tt("dve", kre[:, :], kre[:, :], PEi[:, n_idx, :], ALU.mult, [kre.res, PEi.res], [kre.res])
                B.tt("dve", kim[:, :], kim[:, :], PEi[:, n_idx, :], ALU.mult, [kim.res, PEi.res], [kim.res])
            else:
                B.tt("dve", kim[:, :], t1[:, :], t0[:, :], ALU.add, [t0.res, t1.res], [kim.res])
                B.tt("dve", kre[:, :], kre[:, :], En, ALU.mult, [kre.res, PE_.res], [kre.res])
                B.tt("dve", kim[:, :], kim[:, :], En, ALU.mult, [kim.res, PE_.res], [kim.res])
            B.ts("dve", kim[:, :], kim[:, :], sgn[:, 0:1], None, ALU.mult, None, [kim.res, sgn.res], [kim.res])

        for half in range(2):
            hs_ = slice(half * GH, (half + 1) * GH)
            bc = lambda x: x[:, hs_].unsqueeze(2).to_broadcast([128, GH, M])

            def fill(dst, s, X1, X2, op2):
                B.tt("dve", dst[:, :, s, :], X1[:, hs_, :], bc(kre), ALU.mult, [X1.res, kre.res], [dst.res])
                B.tt("dve", tb0[:, :, :], X2[:, hs_, :], bc(kim), ALU.mult, [X2.res, kim.res], [tb0.res])
                B.tt("dve", dst[:, :, s, :], dst[:, :, s, :], tb0[:, :, :], op2, [dst.res, tb0.res], [dst.res])
            for s in range(8):
                cmul_coef(7 - s, False); fill(W2T, s, Bst, Bsw, ALU.add)
                cmul_coef(1 + s, True); fill(W2Ts, s, Bst, Bsw, ALU.add)
            for t in range(8):
                n = t + 1
                B.tt("dve", kre[:, :], PE_[:, n, :], PC[:, n, :], ALU.mult, [PE_.res, PC.res], [kre.res])
                B.ts("dve", kre[:, :], kre[:, :], sgn[:, 0:1], -1.0, ALU.mult, ALU.mult, [kre.res, sgn.res], [kre.res])
                B.tt("dve", kim[:, :], PE_[:, n, :], PS[:, n, :], ALU.mult, [PE_.res, PS.res], [kim.res])
                fill(W4, t, Cst, Csw, ALU.subtract)
            B.copy("pool", Wo[2][:, :, :], W4[:, :, :, :].rearrange("p g t m -> p g (t m)"), [W4.res], [Wo[2].res])
            for g in range(GH):
                pz = next_psf()
                B.tr(pz[:, 0:128], W2T[:, g, :, :].rearrange("p s m -> p (s m)"), ident_f[:, :], [W2T.res, ident_f.res], [pz.res])
                B.copy("act", Wo[1][:, g, :], pz[:, 0:128], [pz.res], [Wo[1].res])
                pz = next_psf()
                B.mm(pz[:, 0:128], W2Ts[:, g, :, :].rearrange("p s m -> p (s m)"), W4[:, g, :, :].rearrange("p t m -> p (t m)"),
                     True, True, [W2Ts.res, W4.res], [pz.res])
                B.tt("dve", tmpw[:, :], pz[:, 0:128], m1[:, :], ALU.mult, [pz.res, m1.res], [tmpw.res])
                gg = half * GH + g
                B.stt("dve", Wo[0][:, g, :], ident_f[:, :], dcol[:, gg:gg + 1], tmpw[:, :], ALU.mult, ALU.add,
                      [ident_f.res, dcol.res, tmpw.res], [Wo[0].res])
            B.ld(W1d[:, hs_, :], Wo[0][:, :, :], [Wo[0].res], [dr["W1d"]])
            B.ld(W2d[:, hs_, :], Wo[1][:, :, :], [Wo[1].res], [dr["W2d"]])
            B.ld(W4d[:, hs_, :], Wo[2][:, :, :], [Wo[2].res], [dr["W4d"]])
        B.ld(lng[:, :], ln_g[li:li + 1, :].partition_broadcast(128).rearrange("p o d -> p (o d)"), [], [lng.res])
        B.ld(lnb[:, :], ln_b[li:li + 1, :].partition_broadcast(128).rearrange("p o d -> p (o d)"), [], [lnb.res])

    def layer_tiles(li, lastl):
        x_tm = B.sb("x_tm", [128, 8, D], F32)
        a_bf = B.sb("a_bf", [128, 8, D], BF16)
        fmA = B.sb("fmA", [128, 8, 1024], BF16)
        Ug = fmA
        fmZ = B.sb("fmZ", [128, 8, 1024], BF16)
        wbuf = B.sb("wbuf", [128, 8, 1024], BF16)
        tabs = []
        for i in range(2):
            tabs.append(dict(W1=B.sb("W1s", [128, 4, 128], BF16), W2=B.sb("W2s", [128, 4, 128], BF16),
                             W4=B.sb("W4s", [128, 4, 128], BF16), TC=B.sb("TCs", [128, 4, 65], F32),
                             TS=B.sb("TSs", [128, 4, 65], F32), RHO=B.sb("RHOs", [128, 4, 64], F32)))
        Sst = B.sb("Sst", [128, 4, 128], F32)
        Ssw = B.sb("Ssw", [128, 4, 128], F32)
        Srot = B.sb("Srot", [128, 4, 64], F32)
        Stmp = B.sb("Stmp", [128, 4, 64], F32)
        Gs = B.sb("Gs", [128, 4, 65], F32)
        Gsw = B.sb("Gsw", [128, 4, 65], F32)
        Hf = B.sb("Hf", [128, 4, 65], F32)
        Hb = B.sb("Hb", [128, 4, 128], BF16)
        Hst = B.sb("Hst", [128, G], F32)
        H0s = B.sb("H0s", [128, G, 16], F32)
        H0w = B.sb("H0w", [128, G, 16], F32)
        stats = B.sb("stats", [128, 32], F32)
        junk = B.sb("junk", [128, D], BF16)
        tmpf = [B.sb("tmpf", [128, 512], F32) for i in range(2)]
        p_tm = B.sb("p_tm", [128, 8, PLE], F32)
        p_bf = B.sb("p_bf", [128, 8, PLE], BF16)
        pT = B.sb("pT", [128, 2, 1024], BF16)
        stin = B.sb("stin", [64, 16, 128], F32)
        sto = B.sb("sto", [64, 128], F32)
        Ugv = lambda g, NC: Ug[:, g // 8, (g % 8) * 128:(g % 8) * 128 + NC]
        a_gsm = a_bf[:, :, :].rearrange("c s d -> c (s d)").rearrange("c (g s m) -> c g s m", g=G, s=8)
        a_gf = a_bf[:, :, :].rearrange("c s d -> c (s d)").rearrange("c (g f) -> c g f", g=G)

        def to_fm(src_bf, nkc, NC, dst):
            TT = NC * 8
            for k in range(nkc):
                pz = next_psb()
                for s in range(8):
                    B.tr(pz[:, s * NC:(s + 1) * NC], src_bf[0:NC, s, k * 128:(k + 1) * 128], ident_b[0:NC, 0:NC],
                         [src_bf.res, ident_b.res], [pz.res])
                B.copy(B.any_eng(), dst[:, k, 0:TT].rearrange("p (c s) -> p s c", s=8),
                       pz[:, 0:TT].rearrange("p (s c) -> p s c", s=8), [pz.res], [dst.res])

        def load_w(dst, name, l_, nk, c0, ncols):
            src = wb[name][l_].rearrange("(k p) n -> p k n", p=128)
            for k in range(nk):
                B.ld(dst[:, k, 0:ncols], src[:, k, c0:c0 + ncols], [dr[name + "_b"]], [dst.res], eng=("sp", "act")[k % 2])

        def a_tile(src_x, src_p, dst_x, NC, is_sample, first, last_of_seq):
            TT = NC * 8
            ntb = (TT + 511) // 512
            tbw = min(TT, 512)
            B.ld(x_tm[0:NC, :, :], src_x.rearrange("(c s) d -> c s d", s=8), [dr["Xd"], dr["xp"], dr["xs"]], [x_tm.res])
            B.copy("dve", a_bf[0:NC, :, :], x_tm[0:NC, :, :], [x_tm.res], [a_bf.res])
            to_fm(a_bf, 8, NC, fmA)
            if _astop <= 1:
                return
            load_w(wbuf, "w_in_a", li, 8, 1024, 1024)
            for j in range(8):
                for tb in range(ntb):
                    pz = next_psf()
                    for k in range(8):
                        B.mm(pz[:, 0:tbw], wbuf[:, k, j * 128:(j + 1) * 128], fmA[:, k, tb * 512:tb * 512 + tbw],
                             k == 0, k == 7, [fmA.res, wbuf.res], [pz.res])
                    B.act(fmZ[:, j, tb * 512:tb * 512 + tbw], pz[:, 0:tbw], AF.Silu, [pz.res], [fmZ.res])
            if _astop <= 2:
                return
            load_w(wbuf, "w_in_a", li, 8, 0, 1024)
            for s in range(8):
                for nb in range(2):
                    pz = next_psf()
                    for k in range(8):
                        B.mm(pz[0:NC, :], fmA[:, k, s:TT:8], wbuf[:, k, nb * 512:(nb + 1) * 512], k == 0, k == 7,
                             [fmA.res, wbuf.res], [pz.res])
                    B.copy(B.any_eng(), a_gsm[0:NC, nb * 32:(nb + 1) * 32, s, :],
                           pz[0:NC, :].rearrange("c (g m) -> c g m", m=M), [pz.res], [a_bf.res])
            if _astop <= 3:
                return
            for gt in range(8):
                pz = next_psb()
                for g8 in range(8):
                    g = gt * 8 + g8
                    B.tr(pz[:, g8 * NC:(g8 + 1) * NC], a_gf[0:NC, g, :], ident_b[0:NC, 0:NC],
                         [a_bf.res, ident_b.res], [pz.res])
                B.copy(B.any_eng(), Ug[:, gt, :].rearrange("p (g c) -> p g c", g=8)[:, :, 0:NC],
                       pz[:, 0:8 * NC].rearrange("p (g c) -> p g c", g=8), [pz.res], [Ug.res])
            if _astop <= 4:
                return
            if (not is_sample) and first:
                B.kb.op("dve", lambda h: h.memset(Hst[:, :], 0.0), [], [Hst.res])
            for gb in range(G // 4):
                g0 = gb * 4
                gs = slice(g0, g0 + 4)
                tb_ = tabs[gb % 2]
                B.ld(tb_["W1"][:, :, :], W1d[:, gs, :], [dr["W1d"]], [tb_["W1"].res])
                B.ld(tb_["W2"][:, :, :], W2d[:, gs, :], [dr["W2d"]], [tb_["W2"].res], eng="act")
                B.ld(tb_["W4"][:, :, :], W4d[:, gs, :], [dr["W4d"]], [tb_["W4"].res])
                if not is_sample:
                    B.ld(tb_["TC"][:, :, :], TCd[:, gs, :], [dr["TCd"]], [tb_["TC"].res], eng="act")
                    B.ld(tb_["TS"][:, :, :], TSd[:, gs, :], [dr["TSd"]], [tb_["TS"].res])
                    B.ld(tb_["RHO"][:, :, :], RHOd[:, gs, :], [dr["RHOd"]], [tb_["RHO"].res], eng="act")
                TC, TS, RHO = tb_["TC"], tb_["TS"], tb_["RHO"]
                pS = psf[4]
                for g4 in range(4):
                    B.mm(pS[:, g4 * NC:(g4 + 1) * NC], tb_["W2"][:, g4, :], Ugv(g0 + g4, NC), True, True,
                         [tb_["W2"].res, Ug.res], [pS.res])
                pSv = pS[:, 0:4 * NC].rearrange("p (g c) -> p g c", g=4)
                if is_sample:
                    bc = lambda x: x[:, gs].unsqueeze(2).to_broadcast([128, 4, NC])
                    B.tt("dve", Sst[:, :, 0:NC], H0s[:, gs, 0:NC], bc(A8R), ALU.mult, [H0s.res, A8R.res], [Sst.res])
                    B.tt("dve", Ssw[:, :, 0:NC], H0w[:, gs, 0:NC], bc(A8I), ALU.mult, [H0w.res, A8I.res], [Ssw.res])
                    B.tt("dve", Sst[:, :, 0:NC], Sst[:, :, 0:NC], Ssw[:, :, 0:NC], ALU.add, [Sst.res, Ssw.res], [Sst.res])
                    B.copy("pool", Hb[:, :, 0:NC], H0s[:, gs, 0:NC], [H0s.res], [Hb.res])
                    B.tt("dve", H0s[:, gs, 0:NC], Sst[:, :, 0:NC], pSv, ALU.add, [Sst.res, pS.res, Hb.res], [H0s.res])
                else:
                    B.copy("act", Sst[:, :, :], pSv, [pS.res], [Sst.res])
                    B.copy("act", Ssw[0:64, :, :], pSv[64:128], [pS.res], [Ssw.res])
                    B.copy("act", Ssw[64:128, :, :], pSv[0:64], [pS.res], [Ssw.res])
                    for seg in range(2):
                        cs = slice(seg * 64, seg * 64 + 64)
                        B.tt("dve", Srot[:, :, :], Sst[:, :, cs], TC[:, :, 1:65], ALU.mult, [Sst.res, TC.res], [Srot.res])
                        B.tt("pool", Stmp[:, :, :], Ssw[:, :, cs], TS[:, :, 1:65], ALU.mult, [Ssw.res, TS.res], [Stmp.res])
                        B.tt("dve", Srot[:, :, :], Srot[:, :, :], Stmp[:, :, :], ALU.add, [Srot.res, Stmp.res], [Srot.res])
                        B.copy("dve", Gs[:, :, 0:1], Hst[:, gs].unsqueeze(2), [Hst.res], [Gs.res])
                        B.tt("dve", Stmp[:, :, 0:1], Gs[:, :, 0:1], RHO[:, :, 0:1], ALU.mult, [Gs.res, RHO.res, Srot.res], [Stmp.res])
                        B.tt("dve", Srot[:, :, 0:1], Srot[:, :, 0:1], Stmp[:, :, 0:1], ALU.add, [Srot.res, Stmp.res], [Srot.res])
                        for g4 in range(4):
                            B.kb.op("dve", lambda h, g4=g4, RHO=RHO: h.tensor_tensor_scan(
                                out=Gs[:, g4, 1:65], data0=RHO[:, g4, :], data1=Srot[:, g4, :],
                                initial=0.0, op0=ALU.mult, op1=ALU.add),
                                [RHO.res, Srot.res], [Gs.res])
                        B.copy("pool", Gsw[0:64, :, :], Gs[64:128, :, :], [Gs.res], [Gsw.res])
                        B.copy("pool", Gsw[64:128, :, :], Gs[0:64, :, :], [Gs.res], [Gsw.res])
                        B.tt("dve", Hf[:, :, :], Gs[:, :, :], TC[:, :, 0:65], ALU.mult, [Gs.res, TC.res], [Hf.res])
                        B.tt("pool", Gsw[:, :, :], Gsw[:, :, :], TS[:, :, 0:65], ALU.mult, [Gsw.res, TS.res], [Gsw.res])
                        B.tt("dve", Hf[:, :, :], Hf[:, :, :], Gsw[:, :, :], ALU.subtract, [Hf.res, Gsw.res], [Hf.res])
                        B.copy("act", Hb[:, :, cs], Hf[:, :, 0:64], [Hf.res], [Hb.res])
                        B.copy("dve", Hst[:, gs], Hf[:, :, 64], [Hf.res], [Hst.res])
                pY = psf[5]
                for g4 in range(4):
                    B.mm(pY[:, g4 * NC:(g4 + 1) * NC], tb_["W1"][:, g4, :], Ugv(g0 + g4, NC), True, False,
                         [tb_["W1"].res, Ug.res], [pY.res])
                    B.mm(pY[:, g4 * NC:(g4 + 1) * NC], tb_["W4"][:, g4, :], Hb[:, g4, 0:NC], False, True,
                         [tb_["W4"].res, Hb.res], [pY.res])
                gt, g8 = g0 // 8, g0 % 8
                B.ts("dve", Sst[:, :, 0:NC], pY[:, 0:4 * NC].rearrange("p (g c) -> p g c", g=4), -10.0, None, ALU.max, None,
                     [pY.res], [Sst.res])
                B.act(Ug[:, gt, g8 * 128:(g8 + 4) * 128].rearrange("p (g c) -> p g c", g=4)[:, :, 0:NC],
                      Sst[:, :, 0:NC], AF.Gelu, [Sst.res], [Ug.res])
            if _astop <= 5:
                return
            for gt in range(8):
                pz = next_psb()
                for g8 in range(8):
                    g = gt * 8 + g8
                    B.tr(pz[0:NC, g8 * 128:(g8 + 1) * 128], Ugv(g, NC), ident_b[:, :], [Ug.res, ident_b.res], [pz.res])
                B.ts("dve", a_bf[0:NC, :, gt * 128:(gt + 1) * 128].rearrange("c t (g m) -> c g t m", g=8),
                     pz[0:NC, :].rearrange("c (g t m) -> c g t m", g=8, t=8), GSC, None, ALU.mult, None, [pz.res], [a_bf.res])
            to_fm(a_bf, 8, NC, fmA)
            if _os.environ.get("K_FIX") == "add":
                B.ts("dve", fmA[:, :, 0:TT], fmA[:, :, 0:TT], 1.0, None, ALU.add, None, [fmA.res], [fmA.res])
            if _os.environ.get("K_FIX") == "mul":
                B.ts("dve", fmA[:, :, 0:TT], fmA[:, :, 0:TT], float(_os.environ.get("K_SC", "1.0")), None, ALU.mult, None, [fmA.res], [fmA.res])
            if _astop <= 6:
                return
            _m5g = int(_os.environ.get("K_M5", "15"))
            if not (_m5g & 32):
                load_w(wbuf, "w_glu", li, 8, 0, 1024)
            for j in range(8):
                for tb in range(ntb):
                    pz = next_psf()
                    tsl = slice(tb * 512, tb * 512 + tbw)
                    for k in range(8):
                        if _m5g & 16:
                            continue
                        _rhs = fmZ if _os.environ.get("K_RHS") == "z" else fmA
                        _lhs = fmZ if _os.environ.get("K_LHS") == "z" else wbuf
                        B.mm(pz[:, 0:tbw], _lhs[:, k, j * 128:(j + 1) * 128], _rhs[:, k, tsl], k == 0, k == 7,
                             [_rhs.res, _lhs.res], [pz.res])
                    tf = tmpf[(j * ntb + tb) % 2]
                    _m5 = int(_os.environ.get("K_M5", "15"))
                    if _m5 & 1:
                        B.copy("act", tf[:, 0:tbw], pz[:, 0:tbw], [pz.res], [tf.res])
                    if _m5 & 2:
                        B.act(tf[:, 0:tbw], tf[:, 0:tbw], AF.Exp, [tf.res], [tf.res], scale=-1.0 / GSC)
                        B.ts("dve", tf[:, 0:tbw], tf[:, 0:tbw], 1.0, None, ALU.add, None, [tf.res], [tf.res])
                        B.kb.op("dve", lambda h, tf=tf: h.reciprocal(out=tf[:, 0:tbw], in_=tf[:, 0:tbw]), [tf.res], [tf.res])
                    if _m5 & 4:
                        B.tt("dve", tf[:, 0:tbw], tf[:, 0:tbw], fmA[:, j, tsl], ALU.mult, [tf.res, fmA.res], [tf.res])
                    if _m5 & 8:
                        B.tt("dve", fmZ[:, j, tsl], fmZ[:, j, tsl], tf[:, 0:tbw], ALU.mult, [fmZ.res, tf.res], [fmZ.res])
            if _astop <= 7:
                return
            load_w(wbuf, "w_out_a", li, 8, 0, 1024)
            post(src_p, dst_x, NC, fmZ)
            if _pstop < 99:
                return
            if is_sample:
                for c in range(NC):
                    pz = next_psf()
                    B.tr(pz[0:64, 0:128], H0s[:, :, c], ident_f[:, :], [H0s.res, ident_f.res], [pz.res])
                    B.copy("dve", sto[:, :], pz[0:64, 0:128], [pz.res], [sto.res])
                    B.ld(hs_o[li, c], sto[:, :], [sto.res], [dr["hs"]], eng="pool")
            elif last_of_seq:
                pz = next_psf()
                B.tr(pz[0:64, 0:128], Hst[:, :], ident_f[:, :], [Hst.res, ident_f.res], [pz.res])
                B.copy("dve", sto[:, :], pz[0:64, 0:128], [pz.res], [sto.res])
                B.ld(hp_o[li], sto[:, :], [sto.res], [dr["hp"]], eng="pool")

        def post(src_p, dst_x, NC, subT):
            TT = NC * 8
            for s in range(8):
                for nb in range(2):
                    pz = next_psf()
                    for k in range(8):
                        B.mm(pz[0:NC, :], subT[:, k, s:TT:8], wbuf[:, k, nb * 512:(nb + 1) * 512], k == 0, k == 7,
                             [subT.res, wbuf.res], [pz.res])
                    xs_ = x_tm[0:NC, s, nb * 512:(nb + 1) * 512]
                    B.ts("dve", xs_, xs_, float(DN_ALPHA), None, ALU.mult, None, [x_tm.res], [x_tm.res])
                    B.stt("dve", xs_, pz[0:NC, :], 1.0 / GSC, xs_, ALU.mult, ALU.add, [x_tm.res, pz.res], [x_tm.res])
            if _pstop <= 1:
                return
            B.kb.op("dve", lambda h: h.reduce_sum(out=stats[0:NC, 0:8], in_=x_tm[0:NC, :, :], axis=AX.X), [x_tm.res], [stats.res])
            for s in range(8):
                B.act(junk[0:NC, :], x_tm[0:NC, s, :], AF.Square, [x_tm.res], [junk.res, stats.res], accum_out=stats[0:NC, 8 + s:9 + s])
            B.ts("dve", stats[0:NC, 0:8], stats[0:NC, 0:8], 1.0 / D, None, ALU.mult, None, [stats.res], [stats.res])
            B.tt("dve", stats[0:NC, 16:24], stats[0:NC, 0:8], stats[0:NC, 0:8], ALU.mult, [stats.res], [stats.res])
            B.stt("dve", stats[0:NC, 8:16], stats[0:NC, 8:16], 1.0 / D, stats[0:NC, 16:24], ALU.mult, ALU.subtract, [stats.res], [stats.res])
            B.ts("dve", stats[0:NC, 8:16], stats[0:NC, 8:16], LN_EPS, None, ALU.add, None, [stats.res], [stats.res])
            B.act(stats[0:NC, 8:16], stats[0:NC, 8:16], AF.Ln, [stats.res], [stats.res])
            B.act(stats[0:NC, 8:16], stats[0:NC, 8:16], AF.Exp, [stats.res], [stats.res], scale=-0.5)
            for s in range(8):
                B.ts("dve", x_tm[0:NC, s, :], x_tm[0:NC, s, :], stats[0:NC, s:s + 1], stats[0:NC, 8 + s:9 + s], ALU.subtract, ALU.mult,
                     [x_tm.res, stats.res], [x_tm.res])
            gb_ = lambda t: t[0:NC, :].unsqueeze(1).to_broadcast([NC, 8, D])
            B.tt("pool", x_tm[0:NC, :, :], x_tm[0:NC, :, :], gb_(lng), ALU.mult, [x_tm.res, lng.res], [x_tm.res])
            B.tt("dve", x_tm[0:NC, :, :], x_tm[0:NC, :, :], gb_(lnb), ALU.add, [x_tm.res, lnb.res], [x_tm.res])
            if _pstop <= 2:
                return
            B.copy("act", a_bf[0:NC, :, :], x_tm[0:NC, :, :], [x_tm.res], [a_bf.res])
            to_fm(a_bf, 8, NC, fmA)
            B.ld(p_tm[0:NC, :, :], src_p.rearrange("(c s) d -> c s d", s=8), [dr["pp"], dr["ps"]], [p_tm.res])
            B.copy("pool", p_bf[0:NC, :, :], p_tm[0:NC, :, :], [p_tm.res], [p_bf.res])
            to_fm(p_bf, 2, NC, pT)
            if _pstop <= 3:
                return
            load_w(wbuf, "w_pg", li, 8, 0, 1024)
            load_w(fmZ, "w_pe", li, 2, 0, 1024)
            for s in range(8):
                for nb in range(2):
                    pzg = next_psf()
                    for k in range(8):
                        B.mm(pzg[0:NC, :], fmA[:, k, s:TT:8], wbuf[:, k, nb * 512:(nb + 1) * 512], k == 0, k == 7,
                             [fmA.res, wbuf.res], [pzg.res])
                    tf = tmpf[(s * 2 + nb) % 2]
                    B.copy("act", tf[0:NC, :], pzg[0:NC, :], [pzg.res], [tf.res])
                    B.act(tf[0:NC, :], tf[0:NC, :], AF.Exp, [tf.res], [tf.res], scale=-1.0)
                    B.ts("dve", tf[0:NC, :], tf[0:NC, :], 1.0, None, ALU.add, None, [tf.res], [tf.res])
                    B.kb.op("dve", lambda h, tf=tf: h.reciprocal(out=tf[0:NC, :], in_=tf[0:NC, :]), [tf.res], [tf.res])
                    pze = next_psf()
                    for k in range(2):
                        B.mm(pze[0:NC, :], pT[:, k, s:TT:8], fmZ[:, k, nb * 512:(nb + 1) * 512], k == 0, k == 1,
                             [pT.res, fmZ.res], [pze.res])
                    B.tt("dve", tf[0:NC, :], tf[0:NC, :], pze[0:NC, :], ALU.mult, [tf.res, pze.res], [tf.res])
                    xs_ = x_tm[0:NC, s, nb * 512:(nb + 1) * 512]
                    B.tt("dve", xs_, xs_, tf[0:NC, :], ALU.add, [x_tm.res, tf.res], [x_tm.res])
            if _pstop <= 4:
                return
            if _os.environ.get("K_OUTFIX"):
                B.kb.op("dve", lambda h: h.memset(x_tm[0:NC, :, :], 1.0), [], [x_tm.res])
            B.ld(dst_x.rearrange("(c s) d -> c s d", s=8), x_tm[0:NC, :, :], [x_tm.res], [dr["Xd"], dr["yp"], dr["ys"]],
                 eng="pool")

        def load_sample_state():
            B.ld(stin[:, :, 0:64], st_re[li].rearrange("b g p -> g b p"), [], [stin.res])
            B.ld(stin[:, :, 64:128], st_im[li].rearrange("b g p -> g b p"), [], [stin.res])
            for c in range(NS):
                pz = next_psf()
                B.tr(pz[:, 0:64], stin[:, c, :], ident_f[0:64, 0:64], [stin.res, ident_f.res], [pz.res])
                B.copy("dve", H0s[:, :, c], pz[:, 0:64], [pz.res], [H0s.res])
                B.copy("act", H0w[0:64, :, c], pz[64:128, 0:64], [pz.res], [H0w.res])
                B.copy("act", H0w[64:128, :, c], pz[0:64, 0:64], [pz.res], [H0w.res])

        for ti in range(NT):
            src = xp[ti * 1024:(ti + 1) * 1024, :] if li == 0 else Xd[ti * 1024:(ti + 1) * 1024, :]
            dst = (yp if lastl else Xd)[ti * 1024:(ti + 1) * 1024, :]
            a_tile(src, pp[li, ti * 1024:(ti + 1) * 1024, :], dst, 128, False, ti == 0, ti == NT - 1)
        if NS > 0:
            load_sample_state()
            src = xs[:, :] if li == 0 else Xd[L:L + NSTOK, :]
            dst = ys[:, :] if lastl else Xd[L:L + NSTOK, :]
            a_tile(src, ps_[li], dst, NS, True, False, False)

    def kv_phase():
        x_tm = B.sb("x_tm", [128, 8, D], F32)
        a_bf = B.sb("a_bf", [128, 8, D], BF16)
        fmA = B.sb("fmA", [128, 8, 1024], BF16)
        wkv = [B.sb("wkv", [128, 8, 512], BF16) for i in range(2)]
        kv_all = B.sb("kv_all", [128, 8, 512], F32)
        kv_bf = B.sb("kv_bf", [128, 8, 512], BF16)
        KTt = B.sb("KTt", [128, 4, 1024], BF16)
        rt = [B.sb("rt", [128, 8, 8, 8], F32) for i in range(4)]
        rc_p = B.sb("rc_p", [128, NT * 64], F32); rs_p = B.sb("rs_p", [128, NT * 64], F32)
        rc_s = B.sb("rc_s", [128, 64], F32); rs_s = B.sb("rs_s", [128, 64], F32)
        B.ld(rc_p[:, :], rcp[:, :], [], [rc_p.res]); B.ld(rs_p[:, :], rsp[:, :], [], [rs_p.res])
        B.ld(rc_s[:, :], rcs[:, :], [], [rc_s.res]); B.ld(rs_s[:, :], rss[:, :], [], [rs_s.res])
        wsrc = wb["w_kv"][0].rearrange("(k p) n -> p k n", p=128)

        def kv_tile(src_x, NC, ti, is_sample):
            TT = NC * 8
            B.ld(x_tm[0:NC, :, :], src_x.rearrange("(c s) d -> c s d", s=8), [dr["Xd"]], [x_tm.res])
            B.copy("dve", a_bf[0:NC, :, :], x_tm[0:NC, :, :], [x_tm.res], [a_bf.res])
            for k in range(8):
                pz = next_psb()
                for s_ in range(8):
                    B.tr(pz[:, s_ * NC:(s_ + 1) * NC], a_bf[0:NC, s_, k * 128:(k + 1) * 128], ident_b[0:NC, 0:NC],
                         [a_bf.res, ident_b.res], [pz.res])
                B.copy(B.any_eng(), fmA[:, k, 0:TT].rearrange("p (c s) -> p s c", s=8),
                       pz[:, 0:TT].rearrange("p (s c) -> p s c", s=8), [pz.res], [fmA.res])
            if is_sample:
                cosv = rc_s[0:NC, :].rearrange("c (s i) -> c s i", i=8)
                sinv = rs_s[0:NC, :].rearrange("c (s i) -> c s i", i=8)
                crs = [rc_s.res, rs_s.res]
            else:
                cosv = rc_p[0:NC, ti * 64:(ti + 1) * 64].rearrange("c (s i) -> c s i", i=8)
                sinv = rs_p[0:NC, ti * 64:(ti + 1) * 64].rearrange("c (s i) -> c s i", i=8)
                crs = [rc_p.res, rs_p.res]
            cosb = cosv.unsqueeze(2).to_broadcast([NC, 8, 8, 8])
            sinb = sinv.unsqueeze(2).to_broadcast([NC, 8, 8, 8])
            for nb in range(12):
                w_ = wkv[nb % 2]
                for k in range(8):
                    B.ld(w_[:, k, :], wsrc[:, k, nb * 512:(nb + 1) * 512], [dr["w_kv_b"]], [w_.res], eng=("sp", "act")[k % 2])
                for s_ in range(8):
                    pz = next_psf()
                    for k in range(8):
                        B.mm(pz[0:NC, :], fmA[:, k, s_:TT:8], w_[:, k, :], k == 0, k == 7, [fmA.res, w_.res], [pz.res])
                    B.copy(B.any_eng(), kv_all[0:NC, s_, :], pz[0:NC, :], [pz.res], [kv_all.res])
                isk = nb < 6
                g = (nb % 6) // 2
                hh = nb % 2
                if isk:
                    v4 = kv_all[0:NC, :, :].rearrange("c s (h d) -> c s h d", d=64)
                    x1 = v4[:, :, :, 0:8]; x2 = v4[:, :, :, 8:16]
                    r0, r1, r2, r3 = [t[0:NC, :, :, :] for t in rt]
                    B.tt("dve", r0, x1, cosb, ALU.mult, [kv_all.res] + crs, [rt[0].res])
                    B.tt("pool", r1, x2, sinb, ALU.mult, [kv_all.res] + crs, [rt[1].res])
                    B.tt("dve", r2, x2, cosb, ALU.mult, [kv_all.res] + crs, [rt[2].res])
                    B.tt("pool", r3, x1, sinb, ALU.mult, [kv_all.res] + crs, [rt[3].res])
                    B.tt("dve", x1, r0, r1, ALU.subtract, [rt[0].res, rt[1].res, rt[3].res], [kv_all.res])
                    B.tt("dve", x2, r2, r3, ALU.add, [rt[2].res, rt[3].res], [kv_all.res])
                kvi = 0 if isk else 1
                if is_sample:
                    dst = kvs_o[g][:, kvi, hh * 8:(hh + 1) * 8, :].rearrange("(c s) h d -> c s (h d)", s=8)
                    B.ld(dst, kv_all[0:NC, :, :], [kv_all.res], [dr["kvs%d" % g]], eng="pool")
                    B.copy("act", kv_bf[0:NC, :, :], kv_all[0:NC, :, :], [kv_all.res], [kv_bf.res])
                    if not isk:
                        dst = Vnd[:, (nb - 6) * 512:(nb - 5) * 512].rearrange("(c s) n -> c s n", s=8)
                        B.ld(dst, kv_bf[0:NC, :, :], [kv_bf.res], [dr["Vnd"]])
                    else:
                        for j in range(4):
                            pz = next_psb()
                            for s_ in range(8):
                                B.tr(pz[:, s_ * NC:(s_ + 1) * NC], kv_bf[0:NC, s_, j * 128:(j + 1) * 128], ident_b[0:NC, 0:NC],
                                     [kv_bf.res, ident_b.res], [pz.res])
                            B.copy(B.any_eng(), KTt[:, j, 0:TT].rearrange("p (c s) -> p s c", s=8),
                                   pz[:, 0:TT].rearrange("p (s c) -> p s c", s=8), [pz.res], [KTt.res])
                        B.ld(KTnd[g, hh * 4:(hh + 1) * 4, :, :].rearrange("j p t -> p j t"),
                             KTt[:, :, 0:TT], [KTt.res], [dr["KTnd"]])
                else:
                    Wg = min(WS[g], L)
                    lo = L - Wg
                    t0_ = ti * 1024
                    if t0_ + 1024 > lo:
                        c0 = max(0, (lo - t0_) // 8)
                        row0 = t0_ + 8 * c0 - lo
                        nrow = (128 - c0) * 8
                        dst = kvp_o[g][row0:row0 + nrow, kvi, hh * 8:(hh + 1) * 8, :].rearrange("(c s) h d -> c s (h d)", s=8)
                        B.ld(dst, kv_all[c0:128, :, :], [kv_all.res], [dr["kvp%d" % g]], eng="pool")
                    B.copy("act", kv_bf[0:NC, :, :], kv_all[0:NC, :, :], [kv_all.res], [kv_bf.res])
                    if not isk:
                        dst = Vd[ti * 1024:(ti + 1) * 1024, (nb - 6) * 512:(nb - 5) * 512].rearrange("(c s) n -> c s n", s=8)
                        B.ld(dst, kv_bf[0:NC, :, :], [kv_bf.res], [dr["Vd"]])
                    else:
                        for j in range(4):
                            pz = next_psb()
                            for s_ in range(8):
                                B.tr(pz[:, s_ * NC:(s_ + 1) * NC], kv_bf[0:NC, s_, j * 128:(j + 1) * 128], ident_b[0:NC, 0:NC],
                                     [kv_bf.res, ident_b.res], [pz.res])
                            B.copy(B.any_eng(), KTt[:, j, 0:TT].rearrange("p (c s) -> p s c", s=8),
                                   pz[:, 0:TT].rearrange("p (s c) -> p s c", s=8), [pz.res], [KTt.res])
                        B.ld(KTd[g, hh * 4:(hh + 1) * 4, :, ti * 1024:(ti + 1) * 1024].rearrange("j p t -> p j t"),
                             KTt[:, :, :], [KTt.res], [dr["KTd"]])

        for ti in range(NT):
            kv_tile(Xd[ti * 1024:(ti + 1) * 1024, :], 128, ti, False)
        if NS > 0:
            kv_tile(Xd[L:L + NSTOK, :], NS, 0, True)
            cst = [B.sb("cst", [128, 2, 1024], F32) for i in range(2)]
            cbk = [B.sb("cbk", [128, 2, 1024], BF16) for i in range(2)]
            ktc = [B.sb("ktc", [128, 8, 128], BF16) for i in range(2)]
            bi = 0
            for c in range(NS):
                blk = 0
                for g in range(3):
                    d_ = (1, 4, 16)[g]
                    for r in range((1, 4, 8)[g]):
                        st_, cb_, kt_ = cst[bi % 2], cbk[bi % 2], ktc[bi % 2]
                        src = cch[g][c, r:r + d_ * 127 + 1:d_, :, :, :].rearrange("i k h d -> i k (h d)")
                        B.ld(st_[:, :, :], src, [], [st_.res], eng=("sp", "act")[bi % 2])
                        B.copy("dve", cb_[:, 0, :], st_[:, 0, :], [st_.res], [cb_.res])
                        B.copy("pool", cb_[:, 1, :], st_[:, 1, :], [st_.res], [cb_.res])
                        pz = next_psb()
                        for hp in range(8):
                            B.tr(pz[:, hp * 128:(hp + 1) * 128], cb_[:, 0, hp * 128:(hp + 1) * 128], ident_b[:, :],
                                 [cb_.res, ident_b.res], [pz.res])
                        B.copy("act", kt_[:, :, :], pz[:, :].rearrange("p (h k) -> p h k", h=8), [pz.res], [kt_.res])
                        B.ld(KTs[c, :, :, blk * 128:(blk + 1) * 128].rearrange("h p k -> p h k"), kt_[:, :, :], [kt_.res], [dr["KTs"]])
                        B.ld(Vs[c, blk, :, :], cb_[:, 1, :], [cb_.res], [dr["Vs"]], eng="act")
                        blk += 1
                        bi += 1

    def b_layer(j, lastl):
        layer = NA + j
        NU = L // 2048
        xT2 = B.sb("xT2", [128, 8, 2048], BF16)
        fmG = B.sb("fmG", [128, 8, 2048], BF16)
        B.ld(lng[:, :], ln_g[layer:layer + 1, :].partition_broadcast(128).rearrange("p o d -> p (o d)"), [], [lng.res])
        B.ld(lnb[:, :], ln_b[layer:layer + 1, :].partition_broadcast(128).rearrange("p o d -> p (o d)"), [], [lnb.res])
        xsrc = Xd
        for u in range(NU):
            old0 = B.es
            with ExitStack() as es3:
                B.es = es3
                x_tm = B.sb("x_tm", [128, 8, D], F32)
                a_bf = B.sb("a_bf", [128, 8, D], BF16)
                for t2 in range(2):
                    r0_ = u * 2048 + t2 * 1024
                    B.ld(x_tm[:, :, :], xsrc[r0_:r0_ + 1024, :].rearrange("(c s) d -> c s d", s=8), [dr["Xd"]], [x_tm.res])
                    B.copy("dve", a_bf[:, :, :], x_tm[:, :, :], [x_tm.res], [a_bf.res])
                    for k in range(8):
                        pz = next_psb()
                        for s_ in range(8):
                            B.tr(pz[:, s_ * 128:(s_ + 1) * 128], a_bf[:, s_, k * 128:(k + 1) * 128], ident_b[:, :],
                                 [a_bf.res, ident_b.res], [pz.res])
                        B.copy(B.any_eng(), xT2[:, k, t2 * 1024:(t2 + 1) * 1024].rearrange("p (c s) -> p s c", s=8),
                               pz[:, :].rearrange("p (s c) -> p s c", s=8), [pz.res], [xT2.res])
                kb.barrier()
            B.es = old0
            with ExitStack() as es2:
                old_es = B.es; B.es = es2
                ACC = B.sb("ACC", [128, 2, 2048], F32)
                rDl = B.sb("rDl", [64, 2048], F32)
                Opair = B.sb("Opair", [128, 2048], BF16)
                QT = B.sb("QT", [128, 2048], BF16)
                qtmp = [B.sb("qtmp", [128, 512], F32) for i in range(2)]
                KT = B.sb("KT", [128, 4096], BF16)
                Vt = B.sb("Vt", [128, 32, 128], BF16)
                PT = [B.sb("PT", [128, 256], BF16) for i in range(2)]
                zt = B.sb("zt", [128, 2048], BF16)
                wq = B.sb("wq", [128, 8, 128], BF16)
                wqs = B.sb("wqs", [128, 8, 128], BF16)
                wz = B.sb("wz", [128, 8, 128], BF16)
                qc = B.sb("qc", [128, 2048], F32)
                qs_ = B.sb("qs", [128, 2048], F32)
                maskb = B.sb("maskb", [128, 256], BF16)
                maskf = B.sb("maskf", [128, 256], F32)
                onesb = B.sb("onesb", [128, 64], BF16)
                B.ld(maskf[:, :], c_mask[:, :], [], [maskf.res])
                B.copy("dve", maskb[:, :], maskf[:, :], [maskf.res], [maskb.res])
                B.kb.op("dve", lambda h, onesb=onesb: h.memset(onesb[:, :], 1.0), [], [onesb.res])
                B.ld(qc[:, :], rqc[:, u * 2048:(u + 1) * 2048], [], [qc.res])
                B.ld(qs_[:, :], rqs[:, u * 2048:(u + 1) * 2048], [], [qs_.res])
                wqsrc = wb["w_in_b"][j].rearrange("(k p) n -> p k n", p=128)
                wssrc = wb["w_qsw"][j].rearrange("(k p) n -> p k n", p=128)
                for hp in range(8):
                    B.kb.op("pool", lambda h, ACC=ACC: h.memset(ACC[:, :, :], 0.0), [], [ACC.res])
                    for g in range(3):
                        d_ = (1, 4, 16)[g]
                        H_ = 128 * d_
                        nqb = 16 // d_
                        col0 = g * 1024 + hp * 128
                        B.ld(wq[:, :, :], wqsrc[:, :, col0:col0 + 128], [dr["w_in_b_b"]], [wq.res])
                        B.ld(wqs[:, :, :], wssrc[:, :, col0:col0 + 128], [dr["w_qsw_b"]], [wqs.res], eng="act")
                        for tb in range(4):
                            tsl = slice(tb * 512, (tb + 1) * 512)
                            pq = next_psf()
                            for k in range(8):
                                B.mm(pq[:, :], wq[:, k, :], xT2[:, k, tsl], k == 0, k == 7, [wq.res, xT2.res], [pq.res])
                            ps2 = next_psf()
                            for k in range(8):
                                B.mm(ps2[:, :], wqs[:, k, :], xT2[:, k, tsl], k == 0, k == 7, [wqs.res, xT2.res], [ps2.res])
                            q0, q1 = qtmp
                            B.tt("dve", q0[:, :], pq[:, :], qc[:, tsl], ALU.mult, [pq.res, qc.res], [q0.res])
                            B.tt("dve", q1[:, :], ps2[:, :], qs_[:, tsl], ALU.mult, [ps2.res, qs_.res], [q1.res])
                            B.tt("pool", QT[:, tsl], q0[:, :], q1[:, :], ALU.add, [q0.res, q1.res], [QT.res])
                        if u == 0:
                            B.ld(KT[:, H_:H_ + 2048], KTd[g, hp, :, 0:2048], [dr["KTd"]], [KT.res])
                        else:
                            B.ld(KT[:, 0:H_ + 2048], KTd[g, hp, :, u * 2048 - H_:(u + 1) * 2048], [dr["KTd"]], [KT.res])
                        for r in range(d_):
                            jb0 = 0 if u == 0 else -1
                            nblk = nqb - jb0
                            start = u * 2048 + r + d_ * 128 * jb0
                            src = Vd[start:start + d_ * (128 * nblk - 1) + 1:d_, col0:col0 + 128].rearrange("(jj i) c -> i jj c", i=128)
                            b0 = r * (nqb + 1) + (jb0 + 1)
                            B.ld(Vt[:, b0:b0 + nblk, :], src, [dr["Vd"]], [Vt.res], eng=("sp", "act")[r % 2])
                        for hb in range(2):
                            ps_lo, ps_hi = hb * 64, hb * 64 + 64
                            for r in range(d_):
                                for qb in range(nqb):
                                    has_prev = not (u == 0 and qb == 0)
                                    ss_ = lambda a0: slice(a0, a0 + d_ * 127 + 1, d_)
                                    qsl = ss_(r + d_ * 128 * qb)
                                    kc = ss_(H_ + r + d_ * 128 * qb)
                                    kp = ss_(H_ + r + d_ * 128 * (qb - 1))
                                    pS = next_psf()
                                    if has_prev:
                                        B.mm(pS[:, 0:128], KT[ps_lo:ps_hi, kp], QT[ps_lo:ps_hi, qsl], True, True, [KT.res, QT.res], [pS.res])
                                    B.mm(pS[:, 128:256], KT[ps_lo:ps_hi, kc], QT[ps_lo:ps_hi, qsl], True, True, [KT.res, QT.res], [pS.res])
                                    pt = PT[(r * nqb + qb) % 2]
                                    lo_ = 0 if has_prev else 128
                                    B.act(pt[:, lo_:256], pS[:, lo_:256], AF.Exp, [pS.res], [pt.res], scale=0.125)
                                    B.tt("dve", pt[:, lo_:256], pt[:, lo_:256], maskb[:, lo_:256], ALU.mult, [pt.res, maskb.res], [pt.res])
                                    pO = next_psf()
                                    bc_ = r * (nqb + 1) + qb + 1
                                    vsl = slice(hb * 64, hb * 64 + 64)
                                    if has_prev:
                                        B.mm(pO[0:64, 0:128], Vt[:, bc_ - 1, vsl], pt[:, 0:128], True, False, [Vt.res, pt.res], [pO.res])
                                    B.mm(pO[0:64, 0:128], Vt[:, bc_, vsl], pt[:, 128:256], not has_prev, True, [Vt.res, pt.res], [pO.res])
                                    if has_prev:
                                        B.mm(pO[0:64, 128:256], onesb[:, :], pt[:, 0:128], True, False, [onesb.res, pt.res], [pO.res])
                                    B.mm(pO[0:64, 128:256], onesb[:, :], pt[:, 128:256], not has_prev, True, [onesb.res, pt.res], [pO.res])
                                    B.tt("dve", ACC[0:64, hb, qsl], ACC[0:64, hb, qsl], pO[0:64, 0:128], ALU.add, [ACC.res, pO.res], [ACC.res])
                                    B.tt("dve", ACC[64:128, hb, qsl], ACC[64:128, hb, qsl], pO[0:64, 128:256], ALU.add, [ACC.res, pO.res], [ACC.res])
                    B.ld(wz[:, :, :], wqsrc[:, :, 3072 + hp * 128:3072 + (hp + 1) * 128], [dr["w_in_b_b"]], [wz.res])
                    for tb in range(4):
                        tsl = slice(tb * 512, (tb + 1) * 512)
                        pq = next_psf()
                        for k in range(8):
                            B.mm(pq[:, :], wz[:, k, :], xT2[:, k, tsl], k == 0, k == 7, [wz.res, xT2.res], [pq.res])
                        B.act(zt[:, tsl], pq[:, :], AF.Silu, [pq.res], [zt.res])
                    B.kb.op("dve", lambda h, ACC=ACC: h.reciprocal(out=ACC[64:128, :, :], in_=ACC[64:128, :, :]), [ACC.res], [ACC.res])
                    for hb in range(2):
                        B.copy("act", rDl[:, :], ACC[64:128, hb, :], [ACC.res], [rDl.res])
                        B.tt("dve", Opair[hb * 64:hb * 64 + 64, :], ACC[0:64, hb, :], rDl[:, :], ALU.mult, [ACC.res, rDl.res], [Opair.res])
                    B.tt("dve", fmG[:, hp, :], Opair[:, :], zt[:, :], ALU.mult, [Opair.res, zt.res], [fmG.res])
                kb.barrier()
                B.es = old_es
            with ExitStack() as es2:
                old_es = B.es; B.es = es2
                x_tm = B.sb("x_tm", [128, 8, D], F32)
                a_bf = B.sb("a_bf", [128, 8, D], BF16)
                fmA = B.sb("fmA", [128, 8, 1024], BF16)
                wbuf = B.sb("wbuf", [128, 8, 1024], BF16)
                wpe_ = B.sb("wpe", [128, 2, 1024], BF16)
                stats = B.sb("stats", [128, 32], F32)
                junk = B.sb("junk", [128, D], BF16)
                tmpf = [B.sb("tmpf", [128, 512], F32) for i in range(2)]
                p_tm = B.sb("p_tm", [128, 8, PLE], F32)
                p_bf = B.sb("p_bf", [128, 8, PLE], BF16)
                pT = B.sb("pT", [128, 2, 1024], BF16)
                for t2 in range(2):
                    r0_ = u * 2048 + t2 * 1024
                    dst = (yp if lastl else Xd)[r0_:r0_ + 1024, :]
                    B.ld(x_tm[:, :, :], xsrc[r0_:r0_ + 1024, :].rearrange("(c s) d -> c s d", s=8), [dr["Xd"]], [x_tm.res])
                    post_generic(layer, x_tm, a_bf, fmA, wbuf, wpe_, stats, junk, tmpf, p_tm, p_bf, pT,
                                 fmG, t2 * 1024, "w_out_b", j, pp[layer, r0_:r0_ + 1024, :], dst, 128)
                kb.barrier()
                B.es = old_es

    def b_layer_sample(j, lastl):
        layer = NA + j
        NC = NS
        TT = NSTOK
        x_tm = B.sb("x_tm", [128, 8, D], F32)
        a_bf = B.sb("a_bf", [128, 8, D], BF16)
        fmA = B.sb("fmA", [128, 8, 1024], BF16)
        wbuf = B.sb("wbuf", [128, 8, 1024], BF16)
        wpe_ = B.sb("wpe", [128, 2, 1024], BF16)
        stats = B.sb("stats", [128, 32], F32)
        junk = B.sb("junk", [128, D], BF16)
        tmpf = [B.sb("tmpf", [128, 512], F32) for i in range(2)]
        p_tm = B.sb("p_tm", [128, 8, PLE], F32)
        p_bf = B.sb("p_bf", [128, 8, PLE], BF16)
        pT = B.sb("pT", [128, 2, 1024], BF16)
        Qbd = B.sb("Qbd", [128, 24, TT * 2], BF16)
        Zs = B.sb("Zs", [128, 8, TT], BF16)
        Gsm = B.sb("Gsm", [128, 8, TT], BF16)
        KTn = B.sb("KTn", [128, 24, TT], BF16)
        wq = B.sb("wq", [128, 8, 128], BF16)
        wqs = B.sb("wqs", [128, 8, 128], BF16)
        qtmp = [B.sb("qtmp", [128, 128], F32) for i in range(3)]
        qc = B.sb("qcs", [128, 8], F32)
        qs_ = B.sb("qss", [128, 8], F32)
        smf = B.sb("smf", [16, 1688], F32)
        smb = B.sb("smb", [16, 1688], BF16)
        onesb = B.sb("onesb", [128, 128], BF16)
        KT = [B.sb("KTc", [128, 1664], BF16) for i in range(2)]
        Vt = [B.sb("Vtc", [128, 13, 128], BF16) for i in range(2)]
        Vn = [B.sb("Vn", [24, 128], BF16) for i in range(2)]
        Pm = [B.sb("Pm", [16, 1688], BF16) for i in range(2)]
        PTs = [B.sb("PTs", [128, 14, 16], BF16) for i in range(2)]
        rD = B.sb("rD", [128, 16], F32)
        osm = B.sb("osm", [128, 16], F32)
        B.ld(smf[:, :], c_smask[:, :], [], [smf.res])
        B.copy("dve", smb[:, :], smf[:, :], [smf.res], [smb.res])
        B.kb.op("dve", lambda h: h.memset(onesb[:, :], 1.0), [], [onesb.res])
        B.kb.op("pool", lambda h: h.memset(Qbd[:, :, :], 0.0), [], [Qbd.res])
        B.ld(qc[:, :], rqcs[:, :], [], [qc.res])
        B.ld(qs_[:, :], rqss[:, :], [], [qs_.res])
        B.ld(KTn[:, :, :], KTnd.rearrange("g h p t -> p (g h) t"), [dr["KTnd"]], [KTn.res])
        src_x = Xd[L:L + NSTOK, :]
        B.ld(x_tm[0:NC, :, :], src_x.rearrange("(c s) d -> c s d", s=8), [dr["Xd"]], [x_tm.res])
        B.copy("dve", a_bf[0:NC, :, :], x_tm[0:NC, :, :], [x_tm.res], [a_bf.res])
        for k in range(8):
            pz = next_psb()
            for s_ in range(8):
                B.tr(pz[:, s_ * NC:(s_ + 1) * NC], a_bf[0:NC, s_, k * 128:(k + 1) * 128], ident_b[0:NC, 0:NC], [a_bf.res, ident_b.res], [pz.res])
            B.copy(B.any_eng(), fmA[:, k, 0:TT].rearrange("p (c s) -> p s c", s=8), pz[:, 0:TT].rearrange("p (s c) -> p s c", s=8), [pz.res], [fmA.res])
        wqsrc = wb["w_in_b"][j].rearrange("(k p) n -> p k n", p=128)
        wssrc = wb["w_qsw"][j].rearrange("(k p) n -> p k n", p=128)
        qcb = qc[:, :].unsqueeze(1).to_broadcast([128, NC, 8])
        qsb = qs_[:, :].unsqueeze(1).to_broadcast([128, NC, 8])
        for gp in range(24):
            B.ld(wq[:, :, :], wqsrc[:, :, gp * 128:(gp + 1) * 128], [dr["w_in_b_b"]], [wq.res])
            B.ld(wqs[:, :, :], wssrc[:, :, gp * 128:(gp + 1) * 128], [dr["w_qsw_b"]], [wqs.res], eng="act")
            pq = next_psf()
            for k in range(8):
                B.mm(pq[:, 0:TT], wq[:, k, :], fmA[:, k, 0:TT], k == 0, k == 7, [wq.res, fmA.res], [pq.res])
            ps2 = next_psf()
            for k in range(8):
                B.mm(ps2[:, 0:TT], wqs[:, k, :], fmA[:, k, 0:TT], k == 0, k == 7, [wqs.res, fmA.res], [ps2.res])
            q0, q1, q2 = qtmp
            v3 = lambda t: t[:, 0:TT].rearrange("p (c s) -> p c s", s=8)
            B.tt("dve", v3(q0), v3(pq), qcb, ALU.mult, [pq.res, qc.res], [q0.res])
            B.tt("dve", v3(q1), v3(ps2), qsb, ALU.mult, [ps2.res, qs_.res], [q1.res])
            B.tt("dve", q2[:, 0:TT], q0[:, 0:TT], q1[:, 0:TT], ALU.add, [q0.res, q1.res], [q2.res])
            qv = Qbd[:, gp, :].rearrange("p (c hb s) -> p c hb s", hb=2, s=8)
            B.copy("act", qv[0:64, :, 0, :], v3(q2)[0:64], [q2.res], [Qbd.res])
            B.copy("act", qv[64:128, :, 1, :], v3(q2)[64:128], [q2.res], [Qbd.res])
        for hp in range(8):
            B.ld(wq[:, :, :], wqsrc[:, :, 3072 + hp * 128:3072 + (hp + 1) * 128], [dr["w_in_b_b"]], [wq.res])
            pq = next_psf()
            for k in range(8):
                B.mm(pq[:, 0:TT], wq[:, k, :], fmA[:, k, 0:TT], k == 0, k == 7, [wq.res, fmA.res], [pq.res])
            B.act(Zs[:, hp, :], pq[:, 0:TT], AF.Silu, [pq.res], [Zs.res])
        it = 0
        goff = (0, 128, 640)
        gn = (128, 512, 1024)
        for c in range(NC):
            for hp in range(8):
                kt_, vt_, vn_, pm_, pts_ = KT[it % 2], Vt[it % 2], Vn[it % 2], Pm[it % 2], PTs[it % 2]
                B.ld(kt_[:, :], KTs[c, hp, :, :], [dr["KTs"]], [kt_.res])
                B.ld(vt_[:, :, :], Vs[c, :, :, hp * 128:(hp + 1) * 128].rearrange("b k n -> k b n"), [dr["Vs"]], [vt_.res], eng="act")
                for g in range(3):
                    B.ld(vn_[g * 8:(g + 1) * 8, :], Vnd[c * 8:(c + 1) * 8, g * 1024 + hp * 128:g * 1024 + (hp + 1) * 128],
                         [dr["Vnd"]], [vn_.res])
                for g in range(3):
                    qb_ = Qbd[:, g * 8 + hp, c * 16:(c + 1) * 16]
                    for h0 in range(0, gn[g], 512):
                        w_ = min(512, gn[g] - h0)
                        pS = next_psf()
                        B.mm(pS[0:16, 0:w_], qb_, kt_[:, goff[g] + h0:goff[g] + h0 + w_], True, True, [Qbd.res, kt_.res], [pS.res])
                        B.act(pm_[:, goff[g] + h0:goff[g] + h0 + w_], pS[0:16, 0:w_], AF.Exp, [pS.res], [pm_.res], scale=0.125)
                pS = next_psf()
                for g in range(3):
                    B.mm(pS[0:16, g * 8:(g + 1) * 8], Qbd[:, g * 8 + hp, c * 16:(c + 1) * 16], KTn[:, g * 8 + hp, c * 8:(c + 1) * 8],
                         True, True, [Qbd.res, KTn.res], [pS.res])
                B.act(pm_[:, 1664:1688], pS[0:16, 0:24], AF.Exp, [pS.res], [pm_.res], scale=0.125)
                B.tt("dve", pm_[:, :], pm_[:, :], smb[:, :], ALU.mult, [pm_.res, smb.res], [pm_.res])
                pz = next_psb()
                for b_ in range(13):
                    B.tr(pz[:, b_ * 16:(b_ + 1) * 16], pm_[:, b_ * 128:(b_ + 1) * 128], ident_b[0:16, 0:16], [pm_.res, ident_b.res], [pz.res])
                B.tr(pz[0:24, 13 * 16:14 * 16], pm_[:, 1664:1688], ident_b[0:16, 0:16], [pm_.res, ident_b.res], [pz.res])
                B.copy("act", pts_[:, 0:13, :], pz[:, 0:13 * 16].rearrange("p (b q) -> p b q", q=16), [pz.res], [pts_.res])
                B.copy("act", pts_[0:24, 13, :], pz[0:24, 13 * 16:14 * 16], [pz.res], [pts_.res])
                pN = next_psf()
                for b_ in range(13):
                    B.mm(pN[:, 0:16], vt_[:, b_, :], pts_[:, b_, :], b_ == 0, False, [vt_.res, pts_.res], [pN.res])
                B.mm(pN[:, 0:16], vn_[0:24, :], pts_[0:24, 13, :], False, True, [vn_.res, pts_.res], [pN.res])
                for b_ in range(13):
                    B.mm(pN[:, 16:32], onesb[:, :], pts_[:, b_, :], b_ == 0, False, [onesb.res, pts_.res], [pN.res])
                B.mm(pN[:, 16:32], onesb[0:24, :], pts_[0:24, 13, :], False, True, [onesb.res, pts_.res], [pN.res])
                B.kb.op("dve", lambda h, pN=pN: h.reciprocal(out=rD[:, :], in_=pN[:, 16:32]), [pN.res], [rD.res])
                B.tt("dve", osm[:, :], pN[:, 0:16], rD[:, :], ALU.mult, [pN.res, rD.res], [osm.res])
                csl = slice(c * 8, (c + 1) * 8)
                B.tt("dve", Gsm[0:64, hp, csl], osm[0:64, 0:8], Zs[0:64, hp, csl], ALU.mult, [osm.res, Zs.res], [Gsm.res])
                B.tt("dve", Gsm[64:128, hp, csl], osm[64:128, 8:16], Zs[64:128, hp, csl], ALU.mult, [osm.res, Zs.res], [Gsm.res])
                it += 1
        dst = ys[:, :] if lastl else Xd[L:L + NSTOK, :]
        post_generic(layer, x_tm, a_bf, fmA, wbuf, wpe_, stats, junk, tmpf, p_tm, p_bf, pT,
                     Gsm, 0, "w_out_b", j, ps_[layer], dst, NC)

    def post_generic(layer, x_tm, a_bf, fmA, wbuf, wpe_, stats, junk, tmpf, p_tm, p_bf, pT, subT, scol, wname, widx, src_p, dst_x, NC):
        TT = NC * 8
        wsrc = wb[wname][widx].rearrange("(k p) n -> p k n", p=128)
        for k in range(8):
            B.ld(wbuf[:, k, :], wsrc[:, k, :], [dr[wname + "_b"]], [wbuf.res], eng=("sp", "act")[k % 2])
        for s_ in range(8):
            for nb in range(2):
                pz = next_psf()
                for k in range(8):
                    B.mm(pz[0:NC, :], subT[:, k, scol + s_:scol + TT:8], wbuf[:, k, nb * 512:(nb + 1) * 512], k == 0, k == 7,
                         [subT.res, wbuf.res], [pz.res])
                xs_ = x_tm[0:NC, s_, nb * 512:(nb + 1) * 512]
                B.stt("dve", xs_, xs_, float(DN_ALPHA), pz[0:NC, :], ALU.mult, ALU.add, [x_tm.res, pz.res], [x_tm.res])
        B.kb.op("dve", lambda h: h.reduce_sum(out=stats[0:NC, 0:8], in_=x_tm[0:NC, :, :], axis=AX.X), [x_tm.res], [stats.res])
        for s_ in range(8):
            B.act(junk[0:NC, :], x_tm[0:NC, s_, :], AF.Square, [x_tm.res], [junk.res, stats.res], accum_out=stats[0:NC, 8 + s_:9 + s_])
        B.ts("dve", stats[0:NC, 0:8], stats[0:NC, 0:8], 1.0 / D, None, ALU.mult, None, [stats.res], [stats.res])
        B.tt("dve", stats[0:NC, 16:24], stats[0:NC, 0:8], stats[0:NC, 0:8], ALU.mult, [stats.res], [stats.res])
        B.stt("dve", stats[0:NC, 8:16], stats[0:NC, 8:16], 1.0 / D, stats[0:NC, 16:24], ALU.mult, ALU.subtract, [stats.res], [stats.res])
        B.ts("dve", stats[0:NC, 8:16], stats[0:NC, 8:16], LN_EPS, None, ALU.add, None, [stats.res], [stats.res])
        B.act(stats[0:NC, 8:16], stats[0:NC, 8:16], AF.Ln, [stats.res], [stats.res])
        B.act(stats[0:NC, 8:16], stats[0:NC, 8:16], AF.Exp, [stats.res], [stats.res], scale=-0.5)
        for s_ in range(8):
            B.ts("dve", x_tm[0:NC, s_, :], x_tm[0:NC, s_, :], stats[0:NC, s_:s_ + 1], stats[0:NC, 8 + s_:9 + s_], ALU.subtract, ALU.mult,
                 [x_tm.res, stats.res], [x_tm.res])
        gb_ = lambda t: t[0:NC, :].unsqueeze(1).to_broadcast([NC, 8, D])
        B.tt("pool", x_tm[0:NC, :, :], x_tm[0:NC, :, :], gb_(lng), ALU.mult, [x_tm.res, lng.res], [x_tm.res])
        B.tt("dve", x_tm[0:NC, :, :], x_tm[0:NC, :, :], gb_(lnb), ALU.add, [x_tm.res, lnb.res], [x_tm.res])
        B.copy("act", a_bf[0:NC, :, :], x_tm[0:NC, :, :], [x_tm.res], [a_bf.res])
        for k in range(8):
            pz = next_psb()
            for s_ in range(8):
                B.tr(pz[:, s_ * NC:(s_ + 1) * NC], a_bf[0:NC, s_, k * 128:(k + 1) * 128], ident_b[0:NC, 0:NC], [a_bf.res, ident_b.res], [pz.res])
            B.copy(B.any_eng(), fmA[:, k, 0:TT].rearrange("p (c s) -> p s c", s=8), pz[:, 0:TT].rearrange("p (s c) -> p s c", s=8), [pz.res], [fmA.res])
        B.ld(p_tm[0:NC, :, :], src_p.rearrange("(c s) d -> c s d", s=8), [dr["pp"], dr["ps"]], [p_tm.res])
        B.copy("pool", p_bf[0:NC, :, :], p_tm[0:NC, :, :], [p_tm.res], [p_bf.res])
        for k in range(2):
            pz = next_psb()
            for s_ in range(8):
                B.tr(pz[:, s_ * NC:(s_ + 1) * NC], p_bf[0:NC, s_, k * 128:(k + 1) * 128], ident_b[0:NC, 0:NC], [p_bf.res, ident_b.res], [pz.res])
            B.copy(B.any_eng(), pT[:, k, 0:TT].rearrange("p (c s) -> p s c", s=8), pz[:, 0:TT].rearrange("p (s c) -> p s c", s=8), [pz.res], [pT.res])
        wsrc = wb["w_pg"][layer].rearrange("(k p) n -> p k n", p=128)
        for k in range(8):
            B.ld(wbuf[:, k, :], wsrc[:, k, :], [dr["w_pg_b"]], [wbuf.res], eng=("sp", "act")[k % 2])
        wsrc = wb["w_pe"][layer].rearrange("(k p) n -> p k n", p=128)
        for k in range(2):
            B.ld(wpe_[:, k, :], wsrc[:, k, :], [dr["w_pe_b"]], [wpe_.res])
        for s_ in range(8):
            for nb in range(2):
                pzg = next_psf()
                for k in range(8):
                    B.mm(pzg[0:NC, :], fmA[:, k, s_:TT:8], wbuf[:, k, nb * 512:(nb + 1) * 512], k == 0, k == 7, [fmA.res, wbuf.res], [pzg.res])
                tf = tmpf[(s_ * 2 + nb) % 2]
                B.copy("act", tf[0:NC, :], pzg[0:NC, :], [pzg.res], [tf.res])
                B.act(tf[0:NC, :], tf[0:NC, :], AF.Exp, [tf.res], [tf.res], scale=-1.0)
                B.ts("dve", tf[0:NC, :], tf[0:NC, :], 1.0, None, ALU.add, None, [tf.res], [tf.res])
                B.kb.op("dve", lambda h, tf=tf: h.reciprocal(out=tf[0:NC, :], in_=tf[0:NC, :]), [tf.res], [tf.res])
                pze = next_psf()
                for k in range(2):
                    B.mm(pze[0:NC, :], pT[:, k, s_:TT:8], wpe_[:, k, nb * 512:(nb + 1) * 512], k == 0, k == 1, [pT.res, wpe_.res], [pze.res])
                B.tt("dve", tf[0:NC, :], tf[0:NC, :], pze[0:NC, :], ALU.mult, [tf.res, pze.res], [tf.res])
                xs_ = x_tm[0:NC, s_, nb * 512:(nb + 1) * 512]
                B.tt("dve", xs_, xs_, tf[0:NC, :], ALU.add, [x_tm.res, tf.res], [x_tm.res])
        B.ld(dst_x.rearrange("(c s) d -> c s d", s=8), x_tm[0:NC, :, :], [x_tm.res], [dr["Xd"], dr["yp"], dr["ys"]], eng="pool")

    _stop = int(_os.environ.get("K_STOP", "99"))
    n_layers_run = DEPTH if not dbg else dbg
    for li in range(min(NA, n_layers_run)):
        if _stop <= 0:
            break
        with ExitStack() as es:
            B.es = es
            gen_tables(li)
            kb.barrier()
        if _stop <= 1:
            break
        with ExitStack() as es:
            B.es = es
            layer_tiles(li, li == n_layers_run - 1)
            kb.barrier()
        B.es = None
    if n_layers_run > NA:
        with ExitStack() as es:
            B.es = es
            kv_phase()
            kb.barrier()
        B.es = None
        for j in range(n_layers_run - NA):
            with ExitStack() as es:
                B.es = es
                b_layer(j, NA + j == n_layers_run - 1)
                kb.barrier()
            B.es = None
            if NS > 0:
                with ExitStack() as es:
                    B.es = es
                    b_layer_sample(j, NA + j == n_layers_run - 1)
                    kb.barrier()
                B.es = None
    outs = ["yp", "ys", "hp", "hs"] + ["kvp%d" % g for g in range(3)] + ["kvs%d" % g for g in range(3)]
    kb.finish([dr[o] for o in outs])
    kb.emit()
    return nc


def consts():
    c = {}
    c["c_ident"] = np.eye(128, dtype=np.float32)
    idx = np.arange(128)
    c["c_m1"] = ((idx[None, :] // 16) >= (idx[:, None] // 16)).astype(np.float32)
    c["c_sgn"] = np.concatenate([-np.ones(64), np.ones(64)]).astype(np.float32).reshape(128, 1)
    c["c_iota"] = np.broadcast_to(np.arange(65, dtype=np.float32), (128, 65)).copy()
    k_ = idx[:, None]; q_ = idx[None, :]
    c["c_mask"] = np.concatenate([(k_ >= q_), (k_ <= q_)], axis=1).astype(np.float32)
    sm = np.zeros((16, 1688), np.float32)
    goff = (0, 128, 640); dil = (1, 4, 16); nr = (1, 4, 8)
    for hb in range(2):
        for s_ in range(8):
            row = hb * 8 + s_
            for g in range(3):
                d_ = dil[g]
                r = s_ % d_ if g < 2 else s_
                n = s_ // d_
                if g == 2:
                    r, n = s_, 0
                for i in range(128):
                    if i >= n:
                        sm[row, goff[g] + r * 128 + i] = 1.0
                for s2 in range(8):
                    if s2 <= s_ and (s_ - s2) % d_ == 0:
                        sm[row, 1664 + g * 8 + s2] = 1.0
    c["c_smask"] = sm
    return c


def rot_tables(L, NT, past_len=2048):
    inv_freq = (500000.0 ** (-np.arange(0, 16, 2, dtype=np.float32) / 16)).astype(np.float32)
    c = {}
    cidx = np.arange(128)[:, None, None]; tix = np.arange(NT)[None, :, None]; sidx = np.arange(8)[None, None, :]
    pos = (tix * 1024 + 8 * cidx + sidx).astype(np.float32)
    ang = pos[..., None] * inv_freq
    c["rcp"] = np.cos(ang).astype(np.float32).reshape(128, NT * 64)
    c["rsp"] = np.sin(ang).astype(np.float32).reshape(128, NT * 64)
    poss = (past_len + np.arange(8)).astype(np.float32)
    angs = np.broadcast_to((poss[:, None] * inv_freq)[None], (128, 8, 8))
    c["rcs"] = np.cos(angs).astype(np.float32).reshape(128, 64)
    c["rss"] = np.sin(angs).astype(np.float32).reshape(128, 64)

    def fm(posv):
        a = posv[None, :].astype(np.float32) * inv_freq[:, None]
        cosT = np.ones((64, posv.shape[0]), np.float32); sinT = np.zeros((64, posv.shape[0]), np.float32)
        cosT[0:8] = np.cos(a); cosT[8:16] = np.cos(a)
        sinT[0:8] = -np.sin(a); sinT[8:16] = np.sin(a)
        return np.concatenate([cosT, cosT]), np.concatenate([sinT, sinT])
    c["rqc"], c["rqs"] = fm(np.arange(L))
    c["rqcs"], c["rqss"] = fm(past_len + np.arange(8))
    return c


def swap_q_cols(w_in_b):
    perm = np.arange(3072).reshape(48, 64).copy()
    perm[:, 0:8], perm[:, 8:16] = perm[:, 8:16].copy(), perm[:, 0:8].copy()
    return np.ascontiguousarray(w_in_b[:, :, perm.reshape(-1)])


_NC_CACHE = {}


def kernel(x_prompt, x_sample, state_ssm_re, state_ssm_im, cache_kv_w128, cache_kv_w512, cache_kv_w2048,
           p_prompt, p_sample, ln_g, ln_b, w_pe, w_pg, w_in_a, a_re, a_im, log_dt, b_re, b_im, c_re, c_im,
           d_skip, w_glu, w_out_a, w_kv, w_in_b, w_out_b):
    f = lambda a: np.ascontiguousarray(np.asarray(a), dtype=np.float32)
    NCORE = 8
    BATCH, SEQ = x_prompt.shape[0], x_prompt.shape[1]
    DB, DS = x_sample.shape[0], x_sample.shape[1]
    NS = DB // NCORE
    key = (SEQ, NS)
    if key not in _NC_CACHE:
        _NC_CACHE[key] = build(SEQ, NS)
    nc = _NC_CACHE[key]
    shared = dict(consts())
    for k, v in (("ln_g", ln_g), ("ln_b", ln_b), ("w_pe", w_pe), ("w_pg", w_pg), ("w_in_a", w_in_a), ("a_re", a_re),
                 ("a_im", a_im), ("log_dt", log_dt), ("b_re", b_re), ("b_im", b_im), ("c_re", c_re), ("c_im", c_im),
                 ("d_skip", d_skip), ("w_glu", w_glu), ("w_out_a", w_out_a), ("w_kv", w_kv), ("w_in_b", w_in_b),
                 ("w_out_b", w_out_b)):
        shared[k] = f(v)
    shared["w_qsw"] = swap_q_cols(shared["w_in_b"])
    shared.update(rot_tables(SEQ, SEQ // 1024))
    xp = f(x_prompt); pp = f(p_prompt); xs = f(x_sample); ps = f(p_sample)
    sre = f(state_ssm_re); sim = f(state_ssm_im)
    in_maps = []
    for c in range(NCORE):
        b = c % BATCH
        m = dict(shared)
        m["xp"] = xp[b]
        m["pp"] = np.ascontiguousarray(pp[:, b])
        sl = slice(c * NS, (c + 1) * NS)
        m["xs"] = np.ascontiguousarray(xs[sl].reshape(NS * DS, D))
        m["ps"] = np.ascontiguousarray(ps[:, sl].reshape(DEPTH, NS * DS, PLE))
        m["st_re"] = np.ascontiguousarray(sre[:, sl])
        m["st_im"] = np.ascontiguousarray(sim[:, sl])
        m["c128"] = np.ascontiguousarray(cache_kv_w128[sl], dtype=np.float32)
        m["c512"] = np.ascontiguousarray(cache_kv_w512[sl], dtype=np.float32)
        m["c2048"] = np.ascontiguousarray(cache_kv_w2048[sl], dtype=np.float32)
        in_maps.append(m)
    res = run_bass_kernel_spmd(nc, in_maps, core_ids=list(range(NCORE)))
    R = res.results
    y_prompt = np.stack([R[b]["yp"] for b in range(BATCH)]).astype(np.float32)
    y_sample = np.concatenate([R[c]["ys"].reshape(NS, DS, D) for c in range(NCORE)]).astype(np.float32)
    hp = np.stack([R[b]["hp"] for b in range(BATCH)], axis=1)
    hs = np.concatenate([R[c]["hs"] for c in range(NCORE)], axis=1)
    H = 16
    kvp = [np.stack([R[b]["kvp%d" % g] for b in range(BATCH)]).astype(np.float32) for g in range(3)]
    kvs = [np.concatenate([R[c]["kvs%d" % g].reshape(NS, DS, 2, H, 64) for c in range(NCORE)]).astype(np.float32) for g in range(3)]
    return (y_prompt, y_sample,
            np.ascontiguousarray(hp[..., 0:64]), np.ascontiguousarray(hp[..., 64:128]),
            np.ascontiguousarray(hs[..., 0:64]), np.ascontiguousarray(hs[..., 64:128]),
            kvp[0], kvp[1], kvp[2], kvs[0], kvs[1], kvs[2])
```
